# Optimizing a Trainium2 kernel written in Bass

```python
import math
import jax, jax.numpy as jnp
from jax import lax
import numpy as np

D_MODEL = 1024
BATCH = 8
SEQ = 4096
DEPTH = 1
DEC_BATCH = 2
DEC_SEQ = 16384
PAST_LEN = 128

ATTN_WIDTH = D_MODEL // 2
CONV_WIDTH = D_MODEL - ATTN_WIDTH
HEAD_DIM = 64
N_ATTN_HEADS = ATTN_WIDTH // HEAD_DIM
ATTN_CONFIGS = ((128, 1), (512, 4), (2048, 16))
BLK = 64
CONV_K = 3
D_FF = ((int(math.ceil(8 * D_MODEL / 3)) + 255) // 256) * 256
IN_WIDTH = 3 * ATTN_WIDTH + 3 * CONV_WIDTH
ALPHA = (2.0 * DEPTH) ** 0.25
BETA = (8.0 * DEPTH) ** -0.25
LN_EPS = 1e-5
NEG = -1e30

kernel_name = "hymba_dilated_attn_shortconv_deepnorm_encoder"


def _alibi_slopes():
    return jnp.asarray(np.array([2.0 ** (-8.0 * (h + 1) / N_ATTN_HEADS) for h in range(N_ATTN_HEADS)], dtype=np.float32))


def _layernorm(x, g, b):
    xf = x.astype(jnp.float32)
    mu = jnp.mean(xf, axis=-1, keepdims=True)
    var = jnp.mean(jnp.square(xf - mu), axis=-1, keepdims=True)
    y = (xf - mu) * lax.rsqrt(var + LN_EPS) * g.astype(jnp.float32) + b.astype(jnp.float32)
    return y.astype(x.dtype)


def _rmsnorm(x, g):
    xf = x.astype(jnp.float32)
    y = xf * lax.rsqrt(jnp.mean(jnp.square(xf), axis=-1, keepdims=True) + LN_EPS) * g.astype(jnp.float32)
    return y.astype(x.dtype)


def _dilated_band_attention(q, k, v, window, dilation, slopes):
    B, S, H, Dh = q.shape
    d = dilation
    half = window // (2 * d)
    n = S // d
    nb = -(-n // BLK)
    n_pad = nb * BLK

    def to_sub(t):
        return t.reshape(B, n, d, H, Dh).transpose(0, 2, 1, 3, 4)

    qs = jnp.pad(to_sub(q), ((0, 0), (0, 0), (0, n_pad - n), (0, 0), (0, 0)))
    qs = qs.reshape(B, d, nb, BLK, H, Dh)
    pad_kv = ((0, 0), (0, 0), (BLK, n_pad - n + BLK), (0, 0), (0, 0))
    kp = jnp.pad(to_sub(k), pad_kv)
    vp = jnp.pad(to_sub(v), pad_kv)
    kb = jnp.concatenate([kp[:, :, o:o + n_pad].reshape(B, d, nb, BLK, H, Dh) for o in (0, BLK, 2 * BLK)], axis=3)
    vb = jnp.concatenate([vp[:, :, o:o + n_pad].reshape(B, d, nb, BLK, H, Dh) for o in (0, BLK, 2 * BLK)], axis=3)

    s = jnp.einsum('bdnqhc,bdnkhc->bdnhqk', qs, kb, preferred_element_type=jnp.float32) * (1.0 / math.sqrt(Dh))
    qi = jnp.arange(nb)[:, None] * BLK + jnp.arange(BLK)[None, :]
    kj = jnp.arange(nb)[:, None] * BLK - BLK + jnp.arange(3 * BLK)[None, :]
    rel = jnp.abs(qi[:, :, None] - kj[:, None, :])
    valid = ((kj >= 0) & (kj < n))[:, None, :] & (rel <= half)
    dist = (rel * d).astype(jnp.float32)
    bias = -slopes[None, :, None, None] * dist[:, None]
    s = jnp.where(valid[:, None], s + bias, NEG)
    m = jnp.max(s, axis=-1, keepdims=True)
    p = jnp.exp(s - m)
    den = jnp.sum(p, axis=-1, keepdims=True)
    lse = (m + jnp.log(den))[..., 0]
    o = jnp.einsum('bdnhqk,bdnkhc->bdnqhc', p / den, vb.astype(jnp.float32))
    o = o.reshape(B, d, n_pad, H, Dh)[:, :, :n].transpose(0, 2, 1, 3, 4).reshape(B, S, H, Dh)
    lse = lse.transpose(0, 1, 2, 4, 3).reshape(B, d, n_pad, H)[:, :, :n].transpose(0, 2, 1, 3).reshape(B, S, H)
    return o, lse


def _mixed_dilated_attention(q, k, v):
    slopes = _alibi_slopes()
    outs, lses = [], []
    for window, dilation in ATTN_CONFIGS:
        o, l = _dilated_band_attention(q, k, v, window, dilation, slopes)
        outs.append(o)
        lses.append(l)
    w = jax.nn.softmax(jnp.stack(lses, axis=-1), axis=-1)
    o = sum(outs[i] * w[..., i:i + 1] for i in range(len(ATTN_CONFIGS)))
    return o.astype(q.dtype)


def _short_conv(u, gate_b, gate_c, conv_w):
    h = gate_c * u
    hp = jnp.pad(h, ((0, 0), (1, 1), (0, 0)))
    c = conv_w[0] * hp[:, :-2] + conv_w[1] * hp[:, 1:-1] + conv_w[2] * hp[:, 2:]
    return gate_b * c


def _layer(x, w_in, conv_w, g_attn, g_conv, w_o, ln1_g, ln1_b, w_gate, w_up, w_down, ln2_g, ln2_b):
    B, S, _ = x.shape
    p = x @ w_in
    A = ATTN_WIDTH
    C = CONV_WIDTH
    q = p[..., 0:A].reshape(B, S, N_ATTN_HEADS, HEAD_DIM)
    k = p[..., A:2 * A].reshape(B, S, N_ATTN_HEADS, HEAD_DIM)
    v = p[..., 2 * A:3 * A].reshape(B, S, N_ATTN_HEADS, HEAD_DIM)
    u = p[..., 3 * A:3 * A + C]
    gate_b = p[..., 3 * A + C:3 * A + 2 * C]
    gate_c = p[..., 3 * A + 2 * C:3 * A + 3 * C]
    attn = _mixed_dilated_attention(q, k, v).reshape(B, S, A)
    conv = _short_conv(u, gate_b, gate_c, conv_w)
    mix = jnp.concatenate([_rmsnorm(attn, g_attn), _rmsnorm(conv, g_conv)], axis=-1) @ w_o
    x = _layernorm(ALPHA * x + mix, ln1_g, ln1_b)
    ffn = (jax.nn.silu(x @ w_gate) * (x @ w_up)) @ w_down
    x = _layernorm(ALPHA * x + ffn, ln2_g, ln2_b)
    return x


def setup_inputs(seed: int = 0) -> dict:
    key = jax.random.key(seed)
    ks = jax.random.split(key, 16)
    f32 = jnp.float32
    nrm = lambda k, s: jax.random.normal(k, s, dtype=f32)
    return {
        "x_prompt": nrm(ks[0], (BATCH, SEQ, D_MODEL)),
        "x_sample": nrm(ks[1], (DEC_BATCH, DEC_SEQ, D_MODEL)),
        "w_in": nrm(ks[2], (D_MODEL, IN_WIDTH)) * D_MODEL ** -0.5,
        "conv_w": nrm(ks[3], (CONV_K, CONV_WIDTH)) * CONV_K ** -0.5,
        "g_attn": 1.0 + 0.02 * nrm(ks[4], (ATTN_WIDTH,)),
        "g_conv": 1.0 + 0.02 * nrm(ks[5], (CONV_WIDTH,)),
        "w_o": nrm(ks[6], (D_MODEL, D_MODEL)) * (D_MODEL ** -0.5) * BETA,
        "ln1_g": 1.0 + 0.02 * nrm(ks[7], (D_MODEL,)),
        "ln1_b": 0.02 * nrm(ks[8], (D_MODEL,)),
        "w_gate": nrm(ks[9], (D_MODEL, D_FF)) * D_MODEL ** -0.5,
        "w_up": nrm(ks[10], (D_MODEL, D_FF)) * D_MODEL ** -0.5,
        "w_down": nrm(ks[11], (D_FF, D_MODEL)) * (D_FF ** -0.5) * BETA,
        "ln2_g": 1.0 + 0.02 * nrm(ks[12], (D_MODEL,)),
        "ln2_b": 0.02 * nrm(ks[13], (D_MODEL,)),
    }


def reference(x_prompt, x_sample, w_in, conv_w, g_attn, g_conv, w_o, ln1_g, ln1_b, w_gate, w_up, w_down, ln2_g, ln2_b):
    y_prompt = x_prompt
    y_sample = x_sample
    for _ in range(DEPTH):
        y_prompt = _layer(y_prompt, w_in, conv_w, g_attn, g_conv, w_o, ln1_g, ln1_b, w_gate, w_up, w_down, ln2_g, ln2_b)
        y_sample = _layer(y_sample, w_in, conv_w, g_attn, g_conv, w_o, ln1_g, ln1_b, w_gate, w_up, w_down, ln2_g, ln2_b)
    return (y_prompt, y_sample)
```

```python
import math
from contextlib import ExitStack
import numpy as np
import concourse.bass as bass
import concourse.mybir as mybir
from concourse.bass_utils import run_bass_kernel_spmd

F32 = mybir.dt.float32
BF16 = mybir.dt.bfloat16
ALU = mybir.AluOpType
AF = mybir.ActivationFunctionType

D = 1024
DFF = 2816
NF = DFF // 128
NOWN = 8192
WIN = 6144
HALO = 1024
ALPHA = 2.0 ** 0.25
LN_EPS = 1e-5
NEGB = -30000.0
NMB = 117
CFG = (1, 4, 16)


class _Op:
    __slots__ = ("eng", "fn", "deps", "signal", "ordinal", "dma_key", "dma_ord")

    def __init__(self, eng, fn):
        self.eng = eng
        self.fn = fn
        self.deps = []
        self.signal = False
        self.ordinal = 0
        self.dma_key = None
        self.dma_ord = 0


class Prog:
    ENGS = ("tensor", "vector", "scalar", "gpsimd", "sync")

    def __init__(self, nc, es, dma_keys):
        self.nc = nc
        self.pending = {e: [] for e in self.ENGS}
        self.last_w = {}
        self.readers = {}
        self.dma_counts = {k: 0 for k in dma_keys}
        self.esem = {e: es.enter_context(nc.semaphore("pe_" + e)) for e in self.ENGS}
        self.dsem = {k: es.enter_context(nc.semaphore("dq_%d" % i)) for i, k in enumerate(dma_keys)}
        self.ord_count = {e: 0 for e in self.ENGS}
        self.waited = {e: {} for e in self.ENGS}
        self.last_op = {e: None for e in self.ENGS}
        self.last_dma = {}
        self.barrier_deps = {e: [] for e in self.ENGS}
        self.nops = 0

    def op(self, eng, fn, reads=(), writes=(), dma=None):
        o = _Op(eng, fn)
        deps = []
        for r in reads:
            w = self.last_w.get(r)
            if w is not None:
                deps.append(w)
        for wk in writes:
            w = self.last_w.get(wk)
            if w is not None:
                deps.append(w)
            deps.extend(self.readers.get(wk, ()))
        if self.barrier_deps[eng]:
            deps.extend(self.barrier_deps[eng])
            self.barrier_deps[eng] = []
        seen = set()
        for d in deps:
            if id(d) in seen:
                continue
            seen.add(id(d))
            if d.dma_key is None and d.eng == "tensor" and eng == "tensor" and dma is None:
                continue
            o.deps.append(d)
            d.signal = True
        if dma is not None:
            o.dma_key = dma
            self.dma_counts[dma] += 1
            o.dma_ord = self.dma_counts[dma]
            o.signal = True
            self.last_dma[dma] = o
        self.pending[eng].append(o)
        self.last_op[eng] = o
        for r in reads:
            self.readers.setdefault(r, []).append(o)
        for wk in writes:
            self.last_w[wk] = o
            self.readers[wk] = []
        self.nops += 1
        return o

    def flush(self, final=False):
        nc = self.nc
        bar = []
        for e in self.ENGS:
            lo = self.last_op[e]
            if lo is not None and lo.dma_key is None:
                lo.signal = True
                bar.append(lo)
        bar.extend(self.last_dma.values())
        for e in self.ENGS:
            for o in self.pending[e]:
                if o.dma_key is None and o.signal:
                    self.ord_count[e] += 1
                    o.ordinal = self.ord_count[e]
        pend = self.pending
        esem, dsem = self.esem, self.dsem

        def run(en, eng):
            waited = self.waited[en]
            for o in pend[en]:
                need = {}
                for d in o.deps:
                    if d.dma_key is not None:
                        key, val, sem = ("d", d.dma_key), 16 * d.dma_ord, dsem[d.dma_key]
                    else:
                        key, val, sem = ("e", d.eng), d.ordinal, esem[d.eng]
                        assert val > 0
                    if key not in need or need[key][0] < val:
                        need[key] = (val, sem)
                for key, (val, sem) in need.items():
                    if waited.get(key, 0) >= val:
                        continue
                    waited[key] = val
                    eng.wait_ge(sem, val)
                inst = o.fn(eng)
                if o.dma_key is not None:
                    inst.then_inc(dsem[o.dma_key], 16)
                elif o.signal:
                    inst.then_inc(esem[o.eng], 1)
            if final:
                for k, n in self.dma_counts.items():
                    if n and waited.get(("d", k), 0) < 16 * n and en in ("sync",):
                        eng.wait_ge(dsem[k], 16 * n)

        with nc.Block() as block:
            @block.tensor
            def _(e):
                run("tensor", e)

            @block.vector
            def _(e):
                run("vector", e)

            @block.scalar
            def _(e):
                run("scalar", e)

            @block.gpsimd
            def _(e):
                run("gpsimd", e)

            @block.sync
            def _(e):
                run("sync", e)

        self.pending = {e: [] for e in self.ENGS}
        self.last_w = {}
        self.readers = {}
        self.last_dma = {}
        for e in self.ENGS:
            self.barrier_deps[e] = list(bar)


def sl(start, n, step=1):
    return slice(start, start + (n - 1) * step + 1, step)


def build_nc(phases=("1a", "1b", "2"), dbg=False):
    SK = "ExternalOutput" if dbg else "Internal"
    nc = bass.Bass("TRN2", target_bir_lowering=False)
    dt = nc.dram_tensor
    xo = dt("xo", [NOWN, D], F32, kind="ExternalInput").ap()
    xh = dt("xh", [2 * HALO, D], F32, kind="ExternalInput").ap()
    mb_d = dt("mb", [128, 2 * NMB], F32, kind="ExternalInput").ap()
    etab_d = dt("etab", [128, 3 * 4 * 512], F32, kind="ExternalInput").ap()
    ident_d = dt("ident", [128, 128], F32, kind="ExternalInput").ap()
    w_in_d = dt("w_in", [D, 3 * D], F32, kind="ExternalInput").ap()
    w_o_d = dt("w_o", [D, D], F32, kind="ExternalInput").ap()
    w_gate_d = dt("w_gate", [D, DFF], F32, kind="ExternalInput").ap()
    w_up_d = dt("w_up", [D, DFF], F32, kind="ExternalInput").ap()
    w_down_d = dt("w_down", [DFF, D], F32, kind="ExternalInput").ap()
    cw_d = dt("cw", [128, 12], F32, kind="ExternalInput").ap()
    gcol_d = dt("gcol", [128, 8], F32, kind="ExternalInput").ap()
    lnp_d = dt("lnp", [128, 4 * D], F32, kind="ExternalInput").ap()
    lncol_d = dt("lncol", [128, 16], F32, kind="ExternalInput").ap()
    yo = dt("yo", [NOWN, D], F32, kind="ExternalOutput").ap()
    qs = dt("qs", [4, 128, NOWN], BF16, kind=SK).ap()
    ks = dt("ks", [4, 128, 2 * WIN], BF16, kind=SK).ap()
    vs = dt("vs", [4, 128, 2 * WIN], BF16, kind=SK).ap()
    ys = dt("ys", [4, 128, NOWN], BF16, kind=SK).ap()
    ats = dt("ats", [4, 128, NOWN], BF16, kind=SK).ap()
    wgu = dt("wgu", [NF, 128, 2 * 8 * 128], BF16, kind=SK).ap()

    dma_keys = (["xt%d" % i for i in range(8)] + ["st%d" % i for i in range(12)] + ["yst0", "yst1", "zero"]
                + ["c_%s" % n for n in ("mb", "et", "id", "cw", "gc", "ln", "win", "wo", "wo2", "wd", "lc")]
                + ["wgu_c%d" % i for i in range(4)]
                + ["ld_q0", "ld_q1", "ld_k0", "ld_k1", "ld_v0", "ld_v1", "at0", "at1"]
                + ["a2_0", "a2_1", "y2_0", "y2_1", "x2_0", "x2_1", "x2_2", "x2_3", "wr0", "wr1", "wr2", "wr3",
                   "o2_0", "o2_1"])

    with ExitStack() as top:
        P = Prog(nc, top, dma_keys)
        identf = top.enter_context(nc.sbuf_tensor("identf", [128, 128], F32))
        identb = top.enter_context(nc.sbuf_tensor("identb", [128, 128], BF16))
        mbt = top.enter_context(nc.sbuf_tensor("mbt", [128, 2 * NMB], F32))
        cwt = top.enter_context(nc.sbuf_tensor("cwt", [128, 12], F32))
        gct = top.enter_context(nc.sbuf_tensor("gct", [128, 8], F32))
        lnc = top.enter_context(nc.sbuf_tensor("lnc", [128, 16], F32))

        def PS(b):
            return ("ps", b)

        P.op("sync", lambda e: e.dma_start(out=identf[:], in_=ident_d), writes=["identf"], dma="c_id")
        P.op("sync", lambda e: e.dma_start(out=mbt[:], in_=mb_d), writes=["mbt"], dma="c_mb")
        P.op("sync", lambda e: e.dma_start(out=cwt[:], in_=cw_d), writes=["cwt"], dma="c_cw")
        P.op("sync", lambda e: e.dma_start(out=gct[:], in_=gcol_d), writes=["gct"], dma="c_gc")
        P.op("sync", lambda e: e.dma_start(out=lnc[:], in_=lncol_d), writes=["lnc"], dma="c_lc")
        P.op("vector", lambda e: e.tensor_copy(out=identb[:], in_=identf[:]), reads=["identf"], writes=["identb"])

        with ExitStack() as s1:
          if "1a" in phases:
            sb = lambda name, shape, dtp: s1.enter_context(nc.sbuf_tensor("a_" + name, shape, dtp))
            ps = [s1.enter_context(nc.psum_tensor("psa%d" % i, [128, 512], F32)) for i in range(8)]
            win = sb("win", [128, 8, 3 * D], BF16)
            xt = [sb("xt%d" % i, [128, D], F32) for i in range(8)]
            xT = [sb("xT%d" % i, [128, 8, 512], BF16) for i in range(2)]
            st = [sb("st%d" % i, [128, 512], BF16) for i in range(12)]
            gcs = [sb("gcs%d" % i, [128, 512], F32) for i in range(2)]
            hb = [sb("hb%d" % i, [128, 4, 514], F32) for i in range(3)]
            gbb = [sb("gbb%d" % i, [128, 4, 512], F32) for i in range(2)]
            ct = sb("ct", [128, 4, 512], F32)
            t0 = sb("t0", [128, 4, 512], F32)
            yst = [sb("yst%d" % i, [128, 4, 512], BF16) for i in range(2)]
            zt = sb("zt", [128, 1024], BF16)

            for cb in (1, 2, 0, 5, 3, 4):
                P.op("gpsimd", (lambda cb: lambda e: e.dma_start(out=win[:, :, cb * 512:(cb + 1) * 512], in_=w_in_d[:, cb * 512:(cb + 1) * 512].rearrange("(k p) n -> p k n", p=128)))(cb),
                     writes=[("win", cb), "win_ser"], dma="c_win")
            for f in range(NF if "nowgu" not in phases else 0):
                for g, wsrc in enumerate((w_gate_d, w_up_d)):
                    src = wsrc[:, f * 128:(f + 1) * 128].rearrange("(k p) n -> p k n", p=128)
                    dst = wgu[f, :, g * 1024:(g + 1) * 1024].rearrange("p (k n) -> p k n", k=8)
                    P.op("gpsimd", (lambda dst, src: lambda e: e.dma_start(out=dst, in_=src))(dst, src),
                         writes=[("wgu_all", (2 * f + g) % 4)], dma="wgu_c%d" % ((2 * f + g) % 4))
            P.op("vector", lambda e: e.memset(zt[:], 0.0), writes=["zt"])
            for hp in range(4):
                for side in (0, WIN - HALO):
                    for dst_t in (ks, vs):
                        P.op("sync", (lambda d_: lambda e: e.dma_start(out=d_, in_=zt[:]))(dst_t[hp, :, side:side + HALO]),
                             reads=["zt"], dma="zero")

            cnt = {"xt": 0, "st": 0, "tpb": 0, "pjb": 0, "gcs": 0, "yst": 0, "alt": 0}

            def evac(dst, src, psb, reads=(), writes=()):
                cnt["alt"] += 1
                if cnt["alt"] % 2:
                    P.op("scalar", lambda e: e.activation(out=dst, in_=src, func=AF.Copy), reads=reads, writes=[PS(psb)] + list(writes))
                else:
                    P.op("vector", lambda e: e.tensor_copy(out=dst, in_=src), reads=reads, writes=[PS(psb)] + list(writes))

            def conv_finalize(c, n, slot_h, slot_g):
                h = hb[slot_h]
                ysl = cnt["yst"] % 2
                cnt["yst"] += 1
                for j in range(4):
                    hk, gk = ("hb", slot_h, j), ("gbb", slot_g, j)
                    P.op("gpsimd", (lambda j: lambda e: e.tensor_scalar(out=ct[:, j, :], in0=h[:, j, 1:513], scalar1=cwt[:, 3 * j + 1:3 * j + 2], scalar2=0.0, op0=ALU.mult, op1=ALU.add))(j),
                         reads=[hk, "cwt"], writes=[("ct", j)])
                    P.op("gpsimd", (lambda j: lambda e: e.tensor_scalar(out=t0[:, j, :], in0=h[:, j, 0:512], scalar1=cwt[:, 3 * j:3 * j + 1], scalar2=0.0, op0=ALU.mult, op1=ALU.add))(j),
                         reads=[hk, ("hbg", slot_h), "cwt"], writes=[("t0", j)])
                    P.op("gpsimd", (lambda j: lambda e: e.tensor_tensor(out=ct[:, j, :], in0=ct[:, j, :], in1=t0[:, j, :], op=ALU.add))(j),
                         reads=[("t0", j)], writes=[("ct", j)])
                    P.op("gpsimd", (lambda j: lambda e: e.tensor_scalar(out=t0[:, j, :], in0=h[:, j, 2:514], scalar1=cwt[:, 3 * j + 2:3 * j + 3], scalar2=0.0, op0=ALU.mult, op1=ALU.add))(j),
                         reads=[hk, ("hbg", slot_h), "cwt"], writes=[("t0", j)])
                    P.op("gpsimd", (lambda j: lambda e: e.tensor_tensor(out=ct[:, j, :], in0=ct[:, j, :], in1=t0[:, j, :], op=ALU.add))(j),
                         reads=[("t0", j)], writes=[("ct", j)])
                    P.op("gpsimd", (lambda j: lambda e: e.tensor_tensor(out=yst[ysl][:, j, :], in0=ct[:, j, :], in1=gbb[slot_g][:, j, :], op=ALU.mult))(j),
                         reads=[("ct", j), gk], writes=[("yst", ysl)])
                g0 = c * 4096 + (n - 2) * 512
                for j in range(4):
                    P.op("sync", (lambda j: lambda e: e.dma_start(out=ys[j, :, g0:g0 + 512], in_=yst[ysl][:, j, :]))(j),
                         reads=[("yst", ysl)], dma="yst%d" % ysl)

            def tile_src(c, n):
                if 2 <= n <= 9:
                    return xo, c * 4096 + (n - 2) * 512
                if n < 2:
                    return xh, n * 512
                return xh, HALO + (n - 10) * 512

            all_tiles = []
            for c in range(2 if "notiles" not in phases else 0):
                tl = list(range(2, 10)) if c == 0 else list(range(12))
                if "fewtiles" in phases:
                    tl = tl[:2]
                if ("c0only" in phases and c == 1) or ("c1only" in phases and c == 0):
                    continue
                all_tiles += [(c, n) for n in tl]

            def xload(ti):
                if ti >= len(all_tiles):
                    return
                c_, n_ = all_tiles[ti]
                src_, r0_ = tile_src(c_, n_)
                for tt in range(4):
                    xs = (ti * 4 + tt) % 8
                    P.op("sync", (lambda xs, rr, src_: lambda e: e.dma_start(out=xt[xs][:], in_=src_[rr:rr + 128, :]))(xs, r0_ + tt * 128, src_),
                         writes=[("xt", xs)], dma="xt%d" % xs)

            xload(0)
            xload(1)
            tile_ctr = [0]
            for c in range(2 if "notiles" not in phases else 0):
                tiles = list(range(2, 10)) if c == 0 else list(range(12))
                if "fewtiles" in phases:
                    tiles = tiles[:2]
                if "c0only" in phases and c == 1:
                    continue
                if "c1only" in phases and c == 0:
                    continue
                prev_h = None
                for n in tiles:
                    own = 2 <= n <= 9
                    if own:
                        src, r0 = xo, c * 4096 + (n - 2) * 512
                    elif n < 2:
                        src, r0 = xh, n * 512
                    else:
                        src, r0 = xh, HALO + (n - 10) * 512
                    xTs = (c * 12 + n) % 2
                    ti = tile_ctr[0]
                    tile_ctr[0] += 1
                    for tt in range(4):
                        xs = (ti * 4 + tt) % 8
                        for kg in range(2):
                            b = cnt["tpb"] % 2
                            cnt["tpb"] += 1
                            for kk in range(4):
                                k = kg * 4 + kk
                                P.op("tensor", (lambda b, kk, xs, k: lambda e: e.transpose(out=ps[b][:, kk * 128:(kk + 1) * 128], in_=xt[xs][:, k * 128:(k + 1) * 128], identity=identf[:]))(b, kk, xs, k),
                                     reads=[("xt", xs), "identf"], writes=[PS(b)])
                            evac(xT[xTs][:, kg * 4:(kg + 1) * 4, tt * 128:(tt + 1) * 128],
                                 ps[b][:, :].rearrange("p (k t) -> p k t", k=4), b, writes=[("xT", xTs, tt, kg)])
                    xload(ti + 2)
                    need_uc = own or (c == 1 and n in (1, 10))
                    groups = []
                    if own:
                        groups += [("q", j) for j in range(4)]
                    groups += [("k", j) for j in range(4)] + [("v", j) for j in range(4)]
                    if need_uc:
                        hs = n % 3
                        gs = n % 2
                    for kind, j in groups:
                        col0 = {"q": 0, "k": 512, "v": 1024}[kind] + j * 128
                        b = 2 + cnt["pjb"] % 6
                        cnt["pjb"] += 1
                        for k in range(8):
                            P.op("tensor", (lambda b, k, col0, xTs: lambda e: e.matmul(ps[b][:, :], lhsT=win[:, k, col0:col0 + 128], rhs=xT[xTs][:, k, :], start=(k == 0), stop=(k == 7)))(b, k, col0, xTs),
                                 reads=[("win", col0 // 512)] + [("xT", xTs, t_, k // 4) for t_ in range(4)], writes=[PS(b)])
                        ss = cnt["st"] % 12
                        cnt["st"] += 1
                        evac(st[ss][:], ps[b][:, :], b, writes=[("st", ss)])
                        if kind == "q":
                            dst = qs[j, :, c * 4096 + (n - 2) * 512: c * 4096 + (n - 1) * 512]
                        elif kind == "k":
                            dst = ks[j, :, c * WIN + n * 512: c * WIN + (n + 1) * 512]
                        else:
                            dst = vs[j, :, c * WIN + n * 512: c * WIN + (n + 1) * 512]
                        P.op("sync", (lambda dst, ss: lambda e: e.dma_start(out=dst, in_=st[ss][:]))(dst, ss),
                             reads=[("st", ss)], dma="st%d" % ss)
                    if need_uc:
                        for j in range(4):
                            bgc = 2 + cnt["pjb"] % 6
                            cnt["pjb"] += 1
                            col0 = 2560 + j * 128
                            for k in range(8):
                                P.op("tensor", (lambda b, k, col0, xTs: lambda e: e.matmul(ps[b][:, :], lhsT=win[:, k, col0:col0 + 128], rhs=xT[xTs][:, k, :], start=(k == 0), stop=(k == 7)))(bgc, k, col0, xTs),
                                     reads=[("win", col0 // 512)] + [("xT", xTs, t_, k // 4) for t_ in range(4)], writes=[PS(bgc)])
                            gsl = cnt["gcs"] % 2
                            cnt["gcs"] += 1
                            P.op("scalar", (lambda gsl, b: lambda e: e.activation(out=gcs[gsl][:], in_=ps[b][:, :], func=AF.Copy))(gsl, bgc),
                                 writes=[PS(bgc), ("gcs", gsl)])
                            bu = 2 + cnt["pjb"] % 6
                            cnt["pjb"] += 1
                            col0 = 1536 + j * 128
                            for k in range(8):
                                P.op("tensor", (lambda b, k, col0, xTs: lambda e: e.matmul(ps[b][:, :], lhsT=win[:, k, col0:col0 + 128], rhs=xT[xTs][:, k, :], start=(k == 0), stop=(k == 7)))(bu, k, col0, xTs),
                                     reads=[("win", col0 // 512)] + [("xT", xTs, t_, k // 4) for t_ in range(4)], writes=[PS(bu)])
                            P.op("vector", (lambda hs, j, b, gsl: lambda e: e.tensor_tensor(out=hb[hs][:, j, 1:513], in0=ps[b][:, :], in1=gcs[gsl][:], op=ALU.mult))(hs, j, bu, gsl),
                                 reads=[("gcs", gsl)], writes=[PS(bu), ("hb", hs, j)])
                            if own:
                                bg = 2 + cnt["pjb"] % 6
                                cnt["pjb"] += 1
                                col0 = 2048 + j * 128
                                for k in range(8):
                                    P.op("tensor", (lambda b, k, col0, xTs: lambda e: e.matmul(ps[b][:, :], lhsT=win[:, k, col0:col0 + 128], rhs=xT[xTs][:, k, :], start=(k == 0), stop=(k == 7)))(bg, k, col0, xTs),
                                         reads=[("win", col0 // 512)] + [("xT", xTs, t_, k // 4) for t_ in range(4)], writes=[PS(bg)])
                                P.op("scalar", (lambda gs, j, b: lambda e: e.activation(out=gbb[gs][:, j, :], in_=ps[b][:, :], func=AF.Copy))(gs, j, bg),
                                     writes=[PS(bg), ("gbb", gs, j)])
                        hkeys = [("hb", hs, j) for j in range(4)]
                        if prev_h is not None and prev_h[0] == n - 1:
                            ph = prev_h[1]
                            P.op("vector", (lambda hs, ph: lambda e: e.tensor_copy(out=hb[hs][:, :, 0:1], in_=hb[ph][:, :, 512:513]))(hs, ph),
                                 reads=[("hb", ph, j) for j in range(4)], writes=[("hbg", hs)])
                            P.op("vector", (lambda hs, ph: lambda e: e.tensor_copy(out=hb[ph][:, :, 513:514], in_=hb[hs][:, :, 1:2]))(hs, ph),
                                 reads=hkeys, writes=[("hbg", ph)])
                            if 2 <= n - 1 <= 9:
                                conv_finalize(c, n - 1, ph, (n - 1) % 2)
                        else:
                            P.op("vector", (lambda hs: lambda e: e.memset(hb[hs][:, :, 0:1], 0.0))(hs), writes=[("hbg", hs)])
                        prev_h = (n, hs)
                        if c == 0 and n == 9:
                            P.op("vector", (lambda hs: lambda e: e.memset(hb[hs][:, :, 513:514], 0.0))(hs), writes=[("hbg", hs)])
                            conv_finalize(c, 9, hs, 9 % 2)
            P.flush()

        with ExitStack() as s2:
          if "1b" in phases:
            sb = lambda name, shape, dtp: s2.enter_context(nc.sbuf_tensor("b_" + name, shape, dtp))
            ps = [None] * 8
            for i in (0, 1, 4, 5, 6, 7):
                ps[i] = s2.enter_context(nc.psum_tensor("psb%d" % i, [128, 512], F32))
            psb16 = {i: s2.enter_context(nc.psum_tensor("psh%d" % i, [128, 1024], BF16)) for i in (2, 3)}
            et = sb("et", [128, 3 * 4, 2, 256], BF16)
            Qp = [sb("Qp%d" % i, [128, 2, 4096], BF16) for i in range(2)]
            Kp = [sb("Kp%d" % i, [128, WIN], BF16) for i in range(2)]
            Vp = [sb("Vp%d" % i, [128, WIN], BF16) for i in range(2)]
            acc = [sb("acc%d" % i, [128, 4096], F32) for i in range(2)]
            rec = sb("rec", [128, 1024], F32)
            At = [sb("At%d" % i, [128, 4096], BF16) for i in range(2)]
            PT = [sb("PT%d" % i, [128, 2, 256], BF16) for i in range(8)]
            Va = [sb("Va%d" % i, [128, 384], BF16) for i in range(8)]
            P.op("gpsimd", lambda e: e.dma_start(out=et[:].rearrange("p a h n -> p (a h n)"), in_=etab_d), writes=["et"], dma="c_et")
            for i in range(2):
                P.op("gpsimd", (lambda i: lambda e: e.memset(Qp[i][64:128, 0, :], 0.0))(i), writes=[("QpZ", i)])
                P.op("gpsimd", (lambda i: lambda e: e.memset(Qp[i][0:64, 1, :], 0.0))(i), writes=[("QpZ", i)])
            for i in range(8):
                P.op("vector", (lambda i: lambda e: e.memset(Va[i][:, 64:192], 1.0))(i), writes=[("VaO", i)])
            bsmall = "bsmall" in phases
            if bsmall:
                for i in range(2):
                    P.op("vector", (lambda i: lambda e: e.memset(acc[i][:], 1.0))(i), writes=[("acc", i)])
            units = []
            nlp = 0
            nblk_ctr = 0
            for c in range(1 if bsmall else 2):
                for hp in range(1 if bsmall else 4):
                    lp = nlp % 2
                    nlp += 1
                    first_of_hp = True
                    for di, d in enumerate(CFG):
                        Q = 512 if d < 16 else 256
                        nblk = (4096 // d) // Q
                        ntile = Q // 128 + 1
                        mb_base = c * NMB + (0, 33, 69)[di]
                        tiles_per_r = (33, 9, 3)[di]
                        for r in range(1 if bsmall else d):
                            for m in range(1 if bsmall else nblk):
                                qi0 = HALO // d + m * Q
                                bsel = nblk_ctr % 2
                                nblk_ctr += 1
                                for jj in range(ntile):
                                    k0 = qi0 - 64 + 128 * jj
                                    jglob = (qi0 - HALO // d) // 128 + jj
                                    if jj == 0:
                                        c0, N, e0 = 0, 128, 128
                                    elif jj == ntile - 1:
                                        c0, N, e0 = Q - 128, 128, 0
                                    else:
                                        c0, N, e0 = 128 * (jj - 1), 256, 0
                                    units.append(dict(c=c, hp=hp, lp=lp, di=di, d=d, r=r, Q=Q, ntile=ntile, jj=jj, c0=c0, N=N, e0=e0,
                                                      mbc=mb_base + r * tiles_per_r + jglob, kw=k0 * d + r, qo=(qi0 + c0) * d + r - HALO,
                                                      q0=qi0 * d + r - HALO, pa=4 + 2 * bsel, pb=5 + 2 * bsel,
                                                      first_of_hp=first_of_hp, last_of_blk=(jj == ntile - 1), last_of_hp=False))
                                    first_of_hp = False
                    units[-1]["last_of_hp"] = True

            def front(u, U):
                lp, d = U["lp"], U["d"]
                if U["first_of_hp"]:
                    c, hp = U["c"], U["hp"]
                    for h2 in range(2):
                        P.op("sync", (lambda lp, c, hp, h2: lambda e: e.dma_start(out=Qp[lp][h2 * 64:(h2 + 1) * 64, h2, :], in_=qs[hp, h2 * 64:(h2 + 1) * 64, c * 4096:(c + 1) * 4096]))(lp, c, hp, h2), writes=[("Qp", lp)], dma="ld_q%d" % lp)
                    P.op("sync", (lambda lp, c, hp: lambda e: e.dma_start(out=Kp[lp][:], in_=ks[hp, :, c * WIN:(c + 1) * WIN]))(lp, c, hp), writes=[("Kp", lp)], dma="ld_k%d" % lp)
                    P.op("sync", (lambda lp, c, hp: lambda e: e.dma_start(out=Vp[lp][:], in_=vs[hp, :, c * WIN:(c + 1) * WIN]))(lp, c, hp), writes=[("Vp", lp)], dma="ld_v%d" % lp)
                sbk, tbk, vsl, psl = u % 2, 2 + u % 2, u % 8, u % 8
                kw, qo, N, mbc, di, hp, e0 = U["kw"], U["qo"], U["N"], U["mbc"], U["di"], U["hp"], U["e0"]
                P.op("tensor", (lambda tbk, lp, kw, d: lambda e: e.transpose(out=psb16[tbk][:, 0:128], in_=Vp[lp][:, sl(kw, 128, d)], identity=identb[:]))(tbk, lp, kw, d),
                     reads=[("Vp", lp), "identb"], writes=[PS(tbk)])
                if u % 8 == 7:
                    P.op("vector", (lambda vsl, tbk: lambda e: e.tensor_copy(out=Va[vsl][:, :].rearrange("p (a b) -> p a b", b=192)[:, :, 0:64],
                                                                         in_=psb16[tbk][:, 0:128].rearrange("p (a b) -> p a b", b=64)))(vsl, tbk),
                         writes=[PS(tbk), ("Va", vsl)])
                else:
                    P.op("scalar", (lambda vsl, tbk: lambda e: e.activation(out=Va[vsl][:, :].rearrange("p (a b) -> p a b", b=192)[:, :, 0:64],
                                                                         in_=psb16[tbk][:, 0:128].rearrange("p (a b) -> p a b", b=64), func=AF.Copy))(vsl, tbk),
                         writes=[PS(tbk), ("Va", vsl)])
                P.op("tensor", (lambda sbk, lp, kw, qo, N, d: lambda e: e.matmul(
                    ps[sbk][:, 0:2 * N].rearrange("p (h n) -> p h n", h=2), lhsT=Kp[lp][:, sl(kw, 128, d)],
                    rhs=Qp[lp][:, :, sl(qo, N, d)], start=True, stop=True))(sbk, lp, kw, qo, N, d),
                    reads=[("Kp", lp), ("Qp", lp), ("QpZ", lp)], writes=[PS(sbk)])
                P.op("scalar", (lambda psl, sbk, N, mbc: lambda e: e.activation(
                    out=PT[psl][:, :, 0:N], in_=ps[sbk][:, 0:2 * N].rearrange("p (h n) -> p h n", h=2),
                    func=AF.Exp, bias=mbt[:, mbc:mbc + 1], scale=0.125))(psl, sbk, N, mbc),
                    reads=["mbt"], writes=[PS(sbk), ("PT", psl)])
                P.op("vector" if u % 2 == 1 else "gpsimd", (lambda psl, N, di, hp, e0: lambda e: e.tensor_tensor(
                    out=PT[psl][:, :, 0:N], in0=PT[psl][:, :, 0:N], in1=et[:, di * 4 + hp, :, e0:e0 + N], op=ALU.mult))(psl, N, di, hp, e0),
                    reads=["et"], writes=[("PT", psl)])

            def back(u, U):
                vsl, psl = u % 8, u % 8
                c0, N, jj, ntile, d, Q, q0, di = U["c0"], U["N"], U["jj"], U["ntile"], U["d"], U["Q"], U["q0"], U["di"]
                for h2, pacc in ((0, U["pa"]), (1, U["pb"])):
                    P.op("tensor", (lambda pacc, h2, vsl, psl, c0, N, jj, ntile: lambda e: e.matmul(
                        ps[pacc][:, c0:c0 + N], lhsT=Va[vsl][:, h2 * 128:(h2 + 1) * 128], rhs=PT[psl][:, h2, 0:N],
                        start=(jj == 0), stop=(jj == ntile - 1), skip_group_check=True))(pacc, h2, vsl, psl, c0, N, jj, ntile),
                        reads=[("Va", vsl), ("VaO", vsl), ("PT", psl)], writes=[PS(pacc)])
                if U["last_of_blk"]:
                    for h2, pacc in ((0, U["pa"]), (1, U["pb"])):
                        if di == 0:
                            P.op("vector", (lambda h2, pacc, q0, Q, d: lambda e: e.tensor_copy(out=acc[h2][:, sl(q0, Q, d)], in_=ps[pacc][:, 0:Q]))(h2, pacc, q0, Q, d),
                                 writes=[PS(pacc), ("acc", h2)])
                        else:
                            P.op("vector", (lambda h2, pacc, q0, Q, d: lambda e: e.tensor_tensor(out=acc[h2][:, sl(q0, Q, d)], in0=ps[pacc][:, 0:Q], in1=acc[h2][:, sl(q0, Q, d)], op=ALU.add))(h2, pacc, q0, Q, d),
                                 writes=[PS(pacc), ("acc", h2)])
                if U["last_of_hp"]:
                    c, hp = U["c"], U["hp"]
                    asl = U["lp"]
                    for cc in range(4):
                        csl = slice(cc * 1024, (cc + 1) * 1024)
                        P.op("scalar", (lambda csl: lambda e: e.activation(out=rec[0:64, :], in_=acc[0][64:128, csl], func=AF.Ln))(csl), reads=[("acc", 0)], writes=["recA"])
                        P.op("scalar", lambda e: e.activation(out=rec[0:64, :], in_=rec[0:64, :], func=AF.Exp, scale=-1.0), writes=["recA"])
                        P.op("vector", (lambda asl, csl: lambda e: e.tensor_tensor(out=At[asl][0:64, csl], in0=acc[0][0:64, csl], in1=rec[0:64, :], op=ALU.mult))(asl, csl),
                             reads=[("acc", 0), "recA"], writes=[("At", asl)])
                        P.op("scalar", (lambda csl: lambda e: e.activation(out=rec[64:128, :], in_=acc[1][0:64, csl], func=AF.Ln))(csl), reads=[("acc", 1)], writes=["recB"])
                        P.op("scalar", lambda e: e.activation(out=rec[64:128, :], in_=rec[64:128, :], func=AF.Exp, scale=-1.0), writes=["recB"])
                        P.op("vector", (lambda asl, csl: lambda e: e.tensor_tensor(out=At[asl][64:128, csl], in0=acc[1][64:128, csl], in1=rec[64:128, :], op=ALU.mult))(asl, csl),
                             reads=[("acc", 1), "recB"], writes=[("At", asl)])
                    P.op("sync", (lambda asl, hp, c: lambda e: e.dma_start(out=ats[hp, :, c * 4096:(c + 1) * 4096], in_=At[asl][:]))(asl, hp, c),
                         reads=[("At", asl)], dma="at%d" % asl)

            SKEW = 4
            for i in range(len(units) + SKEW):
                if i < len(units):
                    front(i, units[i])
                if i - SKEW >= 0:
                    back(i - SKEW, units[i - SKEW])
            P.flush()

        with ExitStack() as s3:
          if "2" in phases:
            sb = lambda name, shape, dtp: s3.enter_context(nc.sbuf_tensor("c_" + name, shape, dtp))
            ps = [s3.enter_context(nc.psum_tensor("psc%d" % i, [128, 512], F32)) for i in range(8)]
            wd = sb("wd", [128, NF, D], BF16)
            wo = sb("wo", [128, 8, D], BF16)
            lnp = sb("lnp", [128, 4, D], F32)
            wr = [sb("wr%d" % i, [128, 2, 8, 128], BF16) for i in range(4)]
            A2 = [sb("A2_%d" % i, [128, 4, 512], BF16) for i in range(2)]
            Y2 = [sb("Y2_%d" % i, [128, 4, 512], BF16) for i in range(2)]
            sq = sb("sq", [128, 8, 512], BF16)
            ones = sb("ones", [128, 2], BF16)
            x2 = [sb("x2_%d" % i, [128, D], F32) for i in range(3)]
            x1 = sb("x1", [128, 6, D], F32)
            zb = sb("zb", [128, D], F32)
            zb2 = sb("zb2", [128, D], F32)
            wof = zb2
            x1T = sb("x1T", [128, 8, 512], BF16)
            hT = sb("hT", [128, NF, 512], BF16)
            sg = [sb("sg%d" % i, [128, 512], F32) for i in range(2)]
            ob = [sb("ob%d" % i, [128, D], F32) for i in range(2)]
            rs = sb("rs", [128, 8], F32)
            stats = [sb("stats%d" % i, [128, 2, 6], F32) for i in range(2)]
            mv = [sb("mv%d" % i, [128, 2], F32) for i in range(2)]
            sc = [sb("sc%d" % i, [128, 4], F32) for i in range(2)]

            P.op("vector", lambda e: e.memset(ones[:], 1.0), writes=["ones"])
            def phase2_setup_rest():
                for j in range(8):
                    stg, skeys = (zb, [("zb", 0), ("zb", 1)]) if j % 2 == 0 else (zb2, [("zb2", 0), ("zb2", 1)])
                    P.op("sync", (lambda j, stg: lambda e: e.dma_start(out=stg[:], in_=w_o_d[j * 128:(j + 1) * 128, :]))(j, stg), writes=skeys, dma="c_wo" if j % 2 == 0 else "c_wo2")
                    P.op("vector", (lambda j, stg: lambda e: e.tensor_scalar(out=wo[:, j, :], in0=stg[:], scalar1=gct[:, j:j + 1], scalar2=None, op0=ALU.mult))(j, stg),
                         reads=skeys + ["gct"], writes=[("wo", j)])
                P.op("sync", lambda e: e.dma_start(out=lnp[:].rearrange("p a n -> p (a n)"), in_=lnp_d), writes=["lnp"], dma="c_ln")
                for f0 in range(0, NF, 2):
                    P.op("gpsimd", (lambda f0: lambda e: e.dma_start(out=wd[:, f0:f0 + 2, :], in_=w_down_d[f0 * 128:(f0 + 2) * 128, :].rearrange("(f p) n -> p f n", p=128)))(f0),
                         writes=["wd"], dma="c_wd")

            cnt = {"x2": 0, "wr": 0, "gb": 0, "db": 0, "ob": 0, "alt": 0}
            EPS1 = LN_EPS / (ALPHA * ALPHA)

            deferred = []

            def defer(n, fn):
                deferred.append([n, fn])

            def tick():
                for it_ in deferred:
                    it_[0] -= 1
                due = [it_ for it_ in deferred if it_[0] <= 0]
                for it_ in due:
                    deferred.remove(it_)
                    it_[1]()

            def layernorm(src, dst, gi, eps, src_keys, dst_key, gb_eng, si, apply_gb=True, pre=None, post=None, delay=0):
                st_, mv_, sc_ = stats[si], mv[si], sc[si]
                kk = lambda n: (n, si)

                def stage_a():
                    if pre is not None:
                        pre()
                    for hh in range(2):
                        P.op("vector", (lambda hh: lambda e: e.bn_stats(out=st_[:, hh, :], in_=src[:, hh * 512:(hh + 1) * 512]))(hh),
                             reads=src_keys, writes=[kk("stats%d" % hh)])
                    P.op("vector", lambda e: e.bn_aggr(out=mv_[:], in_=st_[:].rearrange("p a s -> p (a s)")), reads=[kk("stats0"), kk("stats1")], writes=[kk("mv")])
                    P.op("vector", lambda e: e.tensor_scalar(out=sc_[:, 0:1], in0=mv_[:, 1:2], scalar1=eps, scalar2=None, op0=ALU.add), reads=[kk("mv")], writes=[kk("sc0")])

                def stage_b():
                    P.op("scalar", lambda e: e.activation(out=sc_[:, 1:2], in_=sc_[:, 0:1], func=AF.Sqrt), reads=[kk("sc0")], writes=[kk("sc1")])

                def stage_c():
                    P.op("vector", lambda e: e.reciprocal(out=sc_[:, 2:3], in_=sc_[:, 1:2]), reads=[kk("sc1")], writes=[kk("sc2")])
                    P.op("vector", lambda e: e.scalar_tensor_tensor(out=sc_[:, 3:4], in0=mv_[:, 0:1], scalar=-1.0, in1=sc_[:, 2:3], op0=ALU.mult, op1=ALU.mult), reads=[kk("mv"), kk("sc2")], writes=[kk("sc3")])
                    P.op("vector", lambda e: e.tensor_scalar(out=dst, in0=src, scalar1=sc_[:, 2:3], scalar2=sc_[:, 3:4], op0=ALU.mult, op1=ALU.add), reads=[kk("sc2"), kk("sc3")] + list(src_keys), writes=[dst_key])
                    if apply_gb:
                        P.op(gb_eng, lambda e: e.tensor_tensor(out=dst, in0=dst, in1=lnp[:, gi, :], op=ALU.mult), reads=["lnp"], writes=[dst_key])
                        P.op(gb_eng, lambda e: e.tensor_tensor(out=dst, in0=dst, in1=lnp[:, gi + 1, :], op=ALU.add), reads=["lnp"], writes=[dst_key])
                    if post is not None:
                        post()

                if delay < 0:
                    stage_a(); stage_b(); stage_c()
                else:
                    if delay == 0:
                        stage_a()
                    else:
                        defer(delay, stage_a)
                    defer(delay + 1, stage_b)
                    defer(delay + 2, stage_c)

            NB = NOWN // 512 if "p2small" not in phases else 2
            x1T_keys = [("x1T", tt) for tt in range(4)]

            def front_start(tb):
                g0, bs = tb * 512, tb % 2
                for j in range(4):
                    P.op("sync", (lambda j, bs, g0: lambda e: e.dma_start(out=A2[bs][:, j, :], in_=ats[j, :, g0:g0 + 512]))(j, bs, g0), writes=[("A2", bs)], dma="a2_%d" % bs)
                    P.op("sync", (lambda j, bs, g0: lambda e: e.dma_start(out=Y2[bs][:, j, :], in_=ys[j, :, g0:g0 + 512]))(j, bs, g0), writes=[("Y2", bs)], dma="y2_%d" % bs)
                x2_load(tb, 0)
                x2_load(tb, 1)

            def front_sq(tb):
                bs = tb % 2
                P.op("scalar", (lambda bs: lambda e: e.activation(out=sq[:, 0:4, :], in_=A2[bs][:], func=AF.Square))(bs), reads=[("A2", bs)], writes=["sqA"])
                P.op("gpsimd", (lambda bs: lambda e: e.tensor_tensor(out=sq[:, 4:8, :], in0=Y2[bs][:], in1=Y2[bs][:], op=ALU.mult))(bs), reads=[("Y2", bs)], writes=["sqB"])

            def front_ssq(tb):
                for tt in range(4):
                    for ab in range(2):
                        for j in range(4):
                            P.op("tensor", (lambda tt, ab, j: lambda e: e.matmul(ps[3][:, tt * 2 + ab:tt * 2 + ab + 1], lhsT=sq[:, ab * 4 + j, tt * 128:(tt + 1) * 128], rhs=ones[:, 0:1], start=(j == 0), stop=(j == 3)))(tt, ab, j),
                                 reads=["sqA" if ab == 0 else "sqB", "ones"], writes=[PS(3)])
                P.op("vector", lambda e: e.tensor_scalar(out=rs[:], in0=ps[3][:, 0:8], scalar1=ALPHA * ALPHA / 512.0, scalar2=LN_EPS * ALPHA * ALPHA, op0=ALU.mult, op1=ALU.add), writes=[PS(3), "rs"])
                P.op("scalar", lambda e: e.activation(out=rs[:], in_=rs[:], func=AF.Sqrt), writes=["rs"])
                P.op("vector", lambda e: e.reciprocal(out=rs[:], in_=rs[:]), writes=["rs"])

            def x2_load(tb, tt):
                xs = (tb * 4 + tt) % 3
                g = tb * 512 + tt * 128
                P.op("sync", (lambda xs, g: lambda e: e.dma_start(out=x2[xs][:], in_=xo[g:g + 128, :]))(xs, g), writes=[("x2", xs)], dma="x2_%d" % xs)

            def front_mix(tb, tt, immediate=False):
                g0, bs = tb * 512, tb % 2
                xs = (tb * 4 + tt) % 3
                for ab, src2 in ((0, A2[bs]), (1, Y2[bs])):
                    for half in range(2):
                        b = ab * 2 + half
                        for j in range(4):
                            P.op("tensor", (lambda b, src2, j, tt, ab, half: lambda e: e.matmul(ps[b][:, :], lhsT=src2[:, j, tt * 128:(tt + 1) * 128], rhs=wo[:, ab * 4 + j, half * 512:(half + 1) * 512], start=(j == 0), stop=(j == 3)))(b, src2, j, tt, ab, half),
                                 reads=[("A2", bs) if ab == 0 else ("Y2", bs), ("wo", ab * 4 + j)], writes=[PS(b)])
                def stt():
                    for half in range(2):
                        hsl = slice(half * 512, (half + 1) * 512)
                        P.op("vector", (lambda half, hsl, xs, tt: lambda e: e.scalar_tensor_tensor(out=zb[:, hsl], in0=ps[half][:, :], scalar=rs[:, 2 * tt:2 * tt + 1], in1=x2[xs][:, hsl], op0=ALU.mult, op1=ALU.add))(half, hsl, xs, tt),
                             reads=["rs", ("x2", xs)], writes=[PS(half), ("zb", half)])
                        P.op("vector", (lambda half, hsl, tt: lambda e: e.scalar_tensor_tensor(out=zb[:, hsl], in0=ps[2 + half][:, :], scalar=rs[:, 2 * tt + 1:2 * tt + 2], in1=zb[:, hsl], op0=ALU.mult, op1=ALU.add))(half, hsl, tt),
                             reads=["rs"], writes=[PS(2 + half), ("zb", half)])
                    if tt + 2 < 4:
                        x2_load(tb, tt + 2)
                xsl = (tb * 4 + tt) % 6
                layernorm(zb[:], x1[:, xsl, :], 0, EPS1, [("zb", 0), ("zb", 1)], ("x1", xsl), "gpsimd", 0, apply_gb=False, pre=stt, delay=(-1 if immediate else 1))

            def front_tr(tb, tt):
                xsl = (tb * 4 + tt) % 6
                for kg in range(2):
                    b = 6 + kg
                    for kk in range(4):
                        k = kg * 4 + kk
                        P.op("tensor", (lambda b, kk, k, xsl: lambda e: e.transpose(out=ps[b][:, kk * 128:(kk + 1) * 128], in_=x1[:, xsl, k * 128:(k + 1) * 128], identity=identf[:]))(b, kk, k, xsl),
                             reads=[("x1", xsl), "identf"], writes=[PS(b)])
                    for kk in range(4):
                        k = kg * 4 + kk
                        cnt["alt"] += 1
                        if cnt["alt"] % 2:
                            P.op("scalar", (lambda b, kk, k, tt: lambda e: e.activation(out=x1T[:, k, tt * 128:(tt + 1) * 128], in_=ps[b][:, kk * 128:(kk + 1) * 128], func=AF.Identity, bias=lnc[:, 8 + k:9 + k], scale=lnc[:, k:k + 1]))(b, kk, k, tt),
                                 reads=["lnc"], writes=[PS(b), ("x1T", tt)])
                        else:
                            P.op("vector", (lambda b, kk, k, tt: lambda e: e.tensor_scalar(out=x1T[:, k, tt * 128:(tt + 1) * 128], in0=ps[b][:, kk * 128:(kk + 1) * 128], scalar1=lnc[:, k:k + 1], scalar2=lnc[:, 8 + k:9 + k], op0=ALU.mult, op1=ALU.add))(b, kk, k, tt),
                                 reads=["lnc"], writes=[PS(b), ("x1T", tt)])
                P.op("gpsimd", (lambda xsl: lambda e: e.tensor_tensor(out=x1[:, xsl, :], in0=x1[:, xsl, :], in1=lnp[:, 0, :], op=ALU.mult))(xsl), reads=["lnp"], writes=[("x1", xsl)])
                P.op("gpsimd", (lambda xsl: lambda e: e.tensor_tensor(out=x1[:, xsl, :], in0=x1[:, xsl, :], in1=lnp[:, 1, :], op=ALU.add))(xsl), reads=["lnp"], writes=[("x1", xsl)])

            def wload(gidx):
                if gidx >= NB * NF:
                    return
                ws, f = gidx % 4, gidx % NF
                P.op("sync", (lambda ws, f: lambda e: e.dma_start(out=wr[ws][:].rearrange("p g k n -> p (g k n)"), in_=wgu[f, :, :]))(ws, f), writes=[("wr", ws)], dma="wr%d" % ws)

            def gateup(tb, hooks=()):
                for f in range(NF):
                    for hf, hfn in hooks:
                        if hf == f:
                            hfn()
                    gidx = tb * NF + f
                    ws = gidx % 4
                    gbk = 4 + 2 * (gidx % 2)
                    for g in range(2):
                        for k in range(8):
                            P.op("tensor", (lambda gbk, g, k, ws: lambda e: e.matmul(ps[gbk + g][:, :], lhsT=wr[ws][:, g, k, :], rhs=x1T[:, k, :], start=(k == 0), stop=(k == 7)))(gbk, g, k, ws),
                                 reads=[("wr", ws)] + x1T_keys, writes=[PS(gbk + g)])
                    wload(gidx + 4)
                    sgs = f % 2
                    P.op("scalar", (lambda sgs, gbk: lambda e: e.activation(out=sg[sgs][:], in_=ps[gbk][:, :], func=AF.Silu))(sgs, gbk), writes=[PS(gbk), ("sg", sgs)])
                    P.op("vector", (lambda sgs, gbk, f: lambda e: e.tensor_tensor(out=hT[:, f, :], in0=ps[gbk + 1][:, :], in1=sg[sgs][:], op=ALU.mult))(sgs, gbk, f),
                         reads=[("sg", sgs)], writes=[PS(gbk + 1), ("hT", f)])
                    tick()

            def down(tb, tt):
                g0 = tb * 512
                for half in range(2):
                    b = 4 + half
                    xsl = (tb * 4 + tt) % 6
                    hsl = slice(half * 512, (half + 1) * 512)
                    for f in range(NF):
                        P.op("tensor", (lambda b, f, tt, half: lambda e: e.matmul(ps[b][:, :], lhsT=hT[:, f, tt * 128:(tt + 1) * 128], rhs=wd[:, f, half * 512:(half + 1) * 512], start=(f == 0), stop=(f == NF - 1)))(b, f, tt, half),
                             reads=[("hT", f), "wd"], writes=[PS(b)])
                        if f == 10 or f == NF - 1:
                            tick()
                    P.op("vector", (lambda b, hsl, xsl: lambda e: e.scalar_tensor_tensor(out=zb2[:, hsl], in0=x1[:, xsl, hsl], scalar=ALPHA, in1=ps[b][:, :], op0=ALU.mult, op1=ALU.add))(b, hsl, xsl),
                         reads=[("x1", xsl)], writes=[PS(b), ("zb2", half)])

            def down_ln(tb, tt):
                g = tb * 512 + tt * 128
                os_ = cnt["ob"] % 2
                cnt["ob"] += 1

                def store():
                    P.op("sync", (lambda os_, g: lambda e: e.dma_start(out=yo[g:g + 128, :], in_=ob[os_][:]))(os_, g), reads=[("ob", os_)], dma="o2_%d" % os_)
                layernorm(zb2[:], ob[os_][:], 2, LN_EPS, [("zb2", 0), ("zb2", 1)], ("ob", os_), "gpsimd", 1, post=store, delay=0)

            front_start(0)
            for g in range(4):
                wload(g)
            phase2_setup_rest()
            front_sq(0)
            front_ssq(0)
            for tt in range(4):
                front_mix(0, tt, immediate=True)
                front_tr(0, tt)
            for tb in range(NB):
                nxt = tb + 1 < NB
                if nxt:
                    gateup(tb, hooks=((1, (lambda tb: lambda: front_start(tb + 1))(tb)), (6, (lambda tb: lambda: front_sq(tb + 1))(tb)), (9, (lambda tb: lambda: front_ssq(tb + 1))(tb)),
                                      (13, (lambda tb: lambda: front_mix(tb + 1, 0))(tb)), (18, (lambda tb: lambda: front_mix(tb + 1, 1))(tb))))
                else:
                    gateup(tb)
                for tt in range(4):
                    down(tb, tt)
                    if nxt:
                        if tt < 2:
                            front_mix(tb + 1, tt + 2)
                        front_tr(tb + 1, tt)
                    down_ln(tb, tt)
            while deferred:
                tick()
            P.flush()
        P.flush(final=True)
    return nc


def _etab():
    slopes = np.array([2.0 ** (-8.0 * (h + 1) / 8) for h in range(8)], dtype=np.float64)
    a = np.arange(128)[:, None]
    b = np.arange(256)[None, :]
    band = ((b - a) >= 0) & ((b - a) <= 128)
    dist = np.abs(b - a - 64).astype(np.float64)
    out = np.zeros((128, 3, 4, 2, 256), dtype=np.float32)
    for di, d in enumerate(CFG):
        for hp in range(4):
            for h2 in range(2):
                out[:, di, hp, h2, :] = np.where(band, np.exp(-slopes[2 * hp + h2] * d * dist), 0.0)
    return np.ascontiguousarray(out.reshape(128, -1))


def _maskbias(valid):
    mb = np.zeros((128, 2 * NMB), dtype=np.float32)
    a = np.arange(128)
    for c in range(2):
        col = c * NMB
        for d, (k00, ntile) in zip(CFG, ((960, 33), (192, 9), (0, 3))):
            for r in range(d):
                for j in range(ntile):
                    w = (k00 + 128 * j + a) * d + r
                    mb[:, col] = np.where(valid[c][w], 0.0, NEGB)
                    col += 1
        assert col == (c + 1) * NMB
    return mb


_NC_CACHE = {}


def kernel(x_prompt, x_sample, w_in, conv_w, g_attn, g_conv, w_o, ln1_g, ln1_b, w_gate, w_up, w_down, ln2_g, ln2_b):
    f32 = lambda a: np.ascontiguousarray(np.asarray(a, dtype=np.float32))
    x_prompt, x_sample = f32(x_prompt), f32(x_sample)
    S2 = x_sample.shape[1]
    etab = _etab()
    ident = np.eye(128, dtype=np.float32)
    cw = f32(np.asarray(conv_w).reshape(3, 4, 128).transpose(2, 1, 0).reshape(128, 12))
    gcol = f32(np.concatenate([np.asarray(g_attn), np.asarray(g_conv)]).reshape(8, 128).T)
    lnp = f32(np.broadcast_to(np.stack([np.asarray(ln1_g), np.asarray(ln1_b), np.asarray(ln2_g), np.asarray(ln2_b)]).reshape(1, 4 * D), (128, 4 * D)))
    lncol = f32(np.concatenate([np.asarray(ln1_g).reshape(8, 128), np.asarray(ln1_b).reshape(8, 128)], axis=0).T)
    shared = {"lncol": lncol, "etab": etab, "ident": ident, "w_in": f32(w_in), "w_o": f32(w_o), "w_gate": f32(w_gate), "w_up": f32(w_up),
              "w_down": f32(w_down), "cw": cw, "gcol": gcol, "lnp": lnp}
    in_maps = []
    for c in range(8):
        s, qd = c // 4, c % 4
        own0 = 4096 * qd
        xo = np.concatenate([x_prompt[c], x_sample[s, own0:own0 + 4096]], axis=0)
        xh = np.zeros((2 * HALO, D), dtype=np.float32)
        valid = np.zeros((2, WIN), dtype=bool)
        valid[0, HALO:HALO + 4096] = True
        valid[1, HALO:HALO + 4096] = True
        if own0 - HALO >= 0:
            xh[0:HALO] = x_sample[s, own0 - HALO:own0]
            valid[1, 0:HALO] = True
        if own0 + 4096 + HALO <= S2:
            xh[HALO:] = x_sample[s, own0 + 4096:own0 + 4096 + HALO]
            valid[1, HALO + 4096:] = True
        m = dict(shared)
        m.update({"xo": np.ascontiguousarray(xo), "xh": xh, "mb": _maskbias(valid)})
        in_maps.append(m)
    if "nc" not in _NC_CACHE:
        _NC_CACHE["nc"] = build_nc()
    res = run_bass_kernel_spmd(_NC_CACHE["nc"], in_maps, core_ids=list(range(8)))
    y_prompt = np.empty_like(x_prompt)
    y_sample = np.empty_like(x_sample)
    for c in range(8):
        yo = np.asarray(res.results[c]["yo"], dtype=np.float32)
        y_prompt[c] = yo[0:4096]
        s, qd = c // 4, c % 4
        y_sample[s, 4096 * qd:4096 * (qd + 1)] = yo[4096:]
    return (y_prompt, y_sample)
```

```python
import math
from contextlib import ExitStack
import numpy as np
import concourse.bass as bass
import concourse.mybir as mybir
from concourse.bass_utils import run_bass_kernel_spmd

F32 = mybir.dt.float32
BF16 = mybir.dt.bfloat16
ALU = mybir.AluOpType
AF = mybir.ActivationFunctionType

D = 1024
DFF = 2816
NF = DFF // 128
NOWN = 8192
WIN = 6144
HALO = 1024
ALPHA = 2.0 ** 0.25
LN_EPS = 1e-5
NEGB = -30000.0
NMB = 117
CFG = (1, 4, 16)


class _Op:
    __slots__ = ("eng", "fn", "deps", "signal", "ordinal", "dma_key", "dma_ord")

    def __init__(self, eng, fn):
        self.eng = eng
        self.fn = fn
        self.deps = []
        self.signal = False
        self.ordinal = 0
        self.dma_key = None
        self.dma_ord = 0


class Prog:
    ENGS = ("tensor", "vector", "scalar", "gpsimd", "sync")

    def __init__(self, nc, es, dma_keys):
        self.nc = nc
        self.pending = {e: [] for e in self.ENGS}
        self.last_w = {}
        self.readers = {}
        self.dma_counts = {k: 0 for k in dma_keys}
        self.esem = {e: es.enter_context(nc.semaphore("pe_" + e)) for e in self.ENGS}
        self.dsem = {k: es.enter_context(nc.semaphore("dq_%d" % i)) for i, k in enumerate(dma_keys)}
        self.ord_count = {e: 0 for e in self.ENGS}
        self.waited = {e: {} for e in self.ENGS}
        self.last_op = {e: None for e in self.ENGS}
        self.last_dma = {}
        self.barrier_deps = {e: [] for e in self.ENGS}
        self.nops = 0

    def op(self, eng, fn, reads=(), writes=(), dma=None):
        o = _Op(eng, fn)
        deps = []
        for r in reads:
            w = self.last_w.get(r)
            if w is not None:
                deps.append(w)
        for wk in writes:
            w = self.last_w.get(wk)
            if w is not None:
                deps.append(w)
            deps.extend(self.readers.get(wk, ()))
        if self.barrier_deps[eng]:
            deps.extend(self.barrier_deps[eng])
            self.barrier_deps[eng] = []
        seen = set()
        for d in deps:
            if id(d) in seen:
                continue
            seen.add(id(d))
            if d.dma_key is None and d.eng == "tensor" and eng == "tensor" and dma is None:
                continue
            o.deps.append(d)
            d.signal = True
        if dma is not None:
            o.dma_key = dma
            self.dma_counts[dma] += 1
            o.dma_ord = self.dma_counts[dma]
            o.signal = True
            self.last_dma[dma] = o
        self.pending[eng].append(o)
        self.last_op[eng] = o
        for r in reads:
            self.readers.setdefault(r, []).append(o)
        for wk in writes:
            self.last_w[wk] = o
            self.readers[wk] = []
        self.nops += 1
        return o

    def flush(self, final=False):
        nc = self.nc
        bar = []
        for e in self.ENGS:
            lo = self.last_op[e]
            if lo is not None and lo.dma_key is None:
                lo.signal = True
                bar.append(lo)
        bar.extend(self.last_dma.values())
        for e in self.ENGS:
            for o in self.pending[e]:
                if o.dma_key is None and o.signal:
                    self.ord_count[e] += 1
                    o.ordinal = self.ord_count[e]
        pend = self.pending
        esem, dsem = self.esem, self.dsem

        def run(en, eng):
            waited = self.waited[en]
            for o in pend[en]:
                need = {}
                for d in o.deps:
                    if d.dma_key is not None:
                        key, val, sem = ("d", d.dma_key), 16 * d.dma_ord, dsem[d.dma_key]
                    else:
                        key, val, sem = ("e", d.eng), d.ordinal, esem[d.eng]
                        assert val > 0
                    if key not in need or need[key][0] < val:
                        need[key] = (val, sem)
                for key, (val, sem) in need.items():
                    if waited.get(key, 0) >= val:
                        continue
                    waited[key] = val
                    eng.wait_ge(sem, val)
                inst = o.fn(eng)
                if o.dma_key is not None:
                    inst.then_inc(dsem[o.dma_key], 16)
                elif o.signal:
                    inst.then_inc(esem[o.eng], 1)
            if final:
                for k, n in self.dma_counts.items():
                    if n and waited.get(("d", k), 0) < 16 * n and en in ("sync",):
                        eng.wait_ge(dsem[k], 16 * n)

        with nc.Block() as block:
            @block.tensor
            def _(e):
                run("tensor", e)

            @block.vector
            def _(e):
                run("vector", e)

            @block.scalar
            def _(e):
                run("scalar", e)

            @block.gpsimd
            def _(e):
                run("gpsimd", e)

            @block.sync
            def _(e):
                run("sync", e)

        self.pending = {e: [] for e in self.ENGS}
        self.last_w = {}
        self.readers = {}
        self.last_dma = {}
        for e in self.ENGS:
            self.barrier_deps[e] = list(bar)


def sl(start, n, step=1):
    return slice(start, start + (n - 1) * step + 1, step)


def build_nc(phases=("1a", "1b", "2"), dbg=False):
    SK = "ExternalOutput" if dbg else "Internal"
    nc = bass.Bass("TRN2", target_bir_lowering=False)
    dt = nc.dram_tensor
    xo = dt("xo", [NOWN, D], F32, kind="ExternalInput").ap()
    xh = dt("xh", [2 * HALO, D], F32, kind="ExternalInput").ap()
    mb_d = dt("mb", [128, 2 * NMB], F32, kind="ExternalInput").ap()
    etab_d = dt("etab", [128, 3 * 4 * 512], F32, kind="ExternalInput").ap()
    ident_d = dt("ident", [128, 128], F32, kind="ExternalInput").ap()
    w_in_d = dt("w_in", [D, 3 * D], F32, kind="ExternalInput").ap()
    w_o_d = dt("w_o", [D, D], F32, kind="ExternalInput").ap()
    w_gate_d = dt("w_gate", [D, DFF], F32, kind="ExternalInput").ap()
    w_up_d = dt("w_up", [D, DFF], F32, kind="ExternalInput").ap()
    w_down_d = dt("w_down", [DFF, D], F32, kind="ExternalInput").ap()
    cw_d = dt("cw", [128, 12], F32, kind="ExternalInput").ap()
    gcol_d = dt("gcol", [128, 8], F32, kind="ExternalInput").ap()
    lnp_d = dt("lnp", [128, 4 * D], F32, kind="ExternalInput").ap()
    lncol_d = dt("lncol", [128, 16], F32, kind="ExternalInput").ap()
    yo = dt("yo", [NOWN, D], F32, kind="ExternalOutput").ap()
    qs = dt("qs", [4, 128, NOWN], BF16, kind=SK).ap()
    ks = dt("ks", [4, 128, 2 * WIN], BF16, kind=SK).ap()
    vs = dt("vs", [4, 128, 2 * WIN], BF16, kind=SK).ap()
    ys = dt("ys", [4, 128, NOWN], BF16, kind=SK).ap()
    ats = dt("ats", [4, 128, NOWN], BF16, kind=SK).ap()
    wgu = dt("wgu", [NF, 128, 2 * 8 * 128], BF16, kind=SK).ap()

    dma_keys = (["xt%d" % i for i in range(8)] + ["st%d" % i for i in range(12)] + ["yst0", "yst1", "zero"]
                + ["c_%s" % n for n in ("mb", "et", "id", "cw", "gc", "ln", "win", "wo", "wo2", "wd", "lc")]
                + ["wgu_c%d" % i for i in range(4)]
                + ["ld_q0", "ld_q1", "ld_k0", "ld_k1", "ld_v0", "ld_v1", "at0", "at1"]
                + ["a2_0", "a2_1", "y2_0", "y2_1", "x2_0", "x2_1", "x2_2", "x2_3", "wr0", "wr1", "wr2", "wr3",
                   "o2_0", "o2_1"])

    with ExitStack() as top:
        P = Prog(nc, top, dma_keys)
        identf = top.enter_context(nc.sbuf_tensor("identf", [128, 128], F32))
        identb = top.enter_context(nc.sbuf_tensor("identb", [128, 128], BF16))
        mbt = top.enter_context(nc.sbuf_tensor("mbt", [128, 2 * NMB], F32))
        cwt = top.enter_context(nc.sbuf_tensor("cwt", [128, 12], F32))
        gct = top.enter_context(nc.sbuf_tensor("gct", [128, 8], F32))
        lnc = top.enter_context(nc.sbuf_tensor("lnc", [128, 16], F32))

        def PS(b):
            return ("ps", b)

        P.op("sync", lambda e: e.dma_start(out=identf[:], in_=ident_d), writes=["identf"], dma="c_id")
        P.op("sync", lambda e: e.dma_start(out=mbt[:], in_=mb_d), writes=["mbt"], dma="c_mb")
        P.op("sync", lambda e: e.dma_start(out=cwt[:], in_=cw_d), writes=["cwt"], dma="c_cw")
        P.op("sync", lambda e: e.dma_start(out=gct[:], in_=gcol_d), writes=["gct"], dma="c_gc")
        P.op("sync", lambda e: e.dma_start(out=lnc[:], in_=lncol_d), writes=["lnc"], dma="c_lc")
        P.op("vector", lambda e: e.tensor_copy(out=identb[:], in_=identf[:]), reads=["identf"], writes=["identb"])

        with ExitStack() as s1:
          if "1a" in phases:
            sb = lambda name, shape, dtp: s1.enter_context(nc.sbuf_tensor("a_" + name, shape, dtp))
            ps = [s1.enter_context(nc.psum_tensor("psa%d" % i, [128, 512], F32)) for i in range(8)]
            win = sb("win", [128, 8, 3 * D], BF16)
            xt = [sb("xt%d" % i, [128, D], F32) for i in range(8)]
            xT = [sb("xT%d" % i, [128, 8, 512], BF16) for i in range(2)]
            st = [sb("st%d" % i, [128, 512], BF16) for i in range(12)]
            gcs = [sb("gcs%d" % i, [128, 512], F32) for i in range(2)]
            hb = [sb("hb%d" % i, [128, 4, 514], F32) for i in range(3)]
            gbb = [sb("gbb%d" % i, [128, 4, 512], F32) for i in range(2)]
            ct = sb("ct", [128, 4, 512], F32)
            t0 = sb("t0", [128, 4, 512], F32)
            yst = [sb("yst%d" % i, [128, 4, 512], BF16) for i in range(2)]
            zt = sb("zt", [128, 1024], BF16)

            for cb in (1, 2, 0, 5, 3, 4):
                P.op("gpsimd", (lambda cb: lambda e: e.dma_start(out=win[:, :, cb * 512:(cb + 1) * 512], in_=w_in_d[:, cb * 512:(cb + 1) * 512].rearrange("(k p) n -> p k n", p=128)))(cb),
                     writes=[("win", cb), "win_ser"], dma="c_win")
            for f in range(NF if "nowgu" not in phases else 0):
                for g, wsrc in enumerate((w_gate_d, w_up_d)):
                    src = wsrc[:, f * 128:(f + 1) * 128].rearrange("(k p) n -> p k n", p=128)
                    dst = wgu[f, :, g * 1024:(g + 1) * 1024].rearrange("p (k n) -> p k n", k=8)
                    P.op("gpsimd", (lambda dst, src: lambda e: e.dma_start(out=dst, in_=src))(dst, src),
                         writes=[("wgu_all", (2 * f + g) % 4)], dma="wgu_c%d" % ((2 * f + g) % 4))
            P.op("vector", lambda e: e.memset(zt[:], 0.0), writes=["zt"])
            for hp in range(4):
                for side in (0, WIN - HALO):
                    for dst_t in (ks, vs):
                        P.op("sync", (lambda d_: lambda e: e.dma_start(out=d_, in_=zt[:]))(dst_t[hp, :, side:side + HALO]),
                             reads=["zt"], dma="zero")

            cnt = {"xt": 0, "st": 0, "tpb": 0, "pjb": 0, "gcs": 0, "yst": 0, "alt": 0}

            def evac(dst, src, psb, reads=(), writes=()):
                cnt["alt"] += 1
                if cnt["alt"] % 2:
                    P.op("scalar", lambda e: e.activation(out=dst, in_=src, func=AF.Copy), reads=reads, writes=[PS(psb)] + list(writes))
                else:
                    P.op("vector", lambda e: e.tensor_copy(out=dst, in_=src), reads=reads, writes=[PS(psb)] + list(writes))

            def conv_finalize(c, n, slot_h, slot_g):
                h = hb[slot_h]
                ysl = cnt["yst"] % 2
                cnt["yst"] += 1
                for j in range(4):
                    hk, gk = ("hb", slot_h, j), ("gbb", slot_g, j)
                    P.op("gpsimd", (lambda j: lambda e: e.tensor_scalar(out=ct[:, j, :], in0=h[:, j, 1:513], scalar1=cwt[:, 3 * j + 1:3 * j + 2], scalar2=0.0, op0=ALU.mult, op1=ALU.add))(j),
                         reads=[hk, "cwt"], writes=[("ct", j)])
                    P.op("gpsimd", (lambda j: lambda e: e.tensor_scalar(out=t0[:, j, :], in0=h[:, j, 0:512], scalar1=cwt[:, 3 * j:3 * j + 1], scalar2=0.0, op0=ALU.mult, op1=ALU.add))(j),
                         reads=[hk, ("hbg", slot_h), "cwt"], writes=[("t0", j)])
                    P.op("gpsimd", (lambda j: lambda e: e.tensor_tensor(out=ct[:, j, :], in0=ct[:, j, :], in1=t0[:, j, :], op=ALU.add))(j),
                         reads=[("t0", j)], writes=[("ct", j)])
                    P.op("gpsimd", (lambda j: lambda e: e.tensor_scalar(out=t0[:, j, :], in0=h[:, j, 2:514], scalar1=cwt[:, 3 * j + 2:3 * j + 3], scalar2=0.0, op0=ALU.mult, op1=ALU.add))(j),
                         reads=[hk, ("hbg", slot_h), "cwt"], writes=[("t0", j)])
                    P.op("gpsimd", (lambda j: lambda e: e.tensor_tensor(out=ct[:, j, :], in0=ct[:, j, :], in1=t0[:, j, :], op=ALU.add))(j),
                         reads=[("t0", j)], writes=[("ct", j)])
                    P.op("gpsimd", (lambda j: lambda e: e.tensor_tensor(out=yst[ysl][:, j, :], in0=ct[:, j, :], in1=gbb[slot_g][:, j, :], op=ALU.mult))(j),
                         reads=[("ct", j), gk], writes=[("yst", ysl)])
                g0 = c * 4096 + (n - 2) * 512
                for j in range(4):
                    P.op("sync", (lambda j: lambda e: e.dma_start(out=ys[j, :, g0:g0 + 512], in_=yst[ysl][:, j, :]))(j),
                         reads=[("yst", ysl)], dma="yst%d" % ysl)

            def tile_src(c, n):
                if 2 <= n <= 9:
                    return xo, c * 4096 + (n - 2) * 512
                if n < 2:
                    return xh, n * 512
                return xh, HALO + (n - 10) * 512

            all_tiles = []
            for c in range(2 if "notiles" not in phases else 0):
                tl = list(range(2, 10)) if c == 0 else list(range(12))
                if "fewtiles" in phases:
                    tl = tl[:2]
                if ("c0only" in phases and c == 1) or ("c1only" in phases and c == 0):
                    continue
                all_tiles += [(c, n) for n in tl]

            def xload(ti):
                if ti >= len(all_tiles):
                    return
                c_, n_ = all_tiles[ti]
                src_, r0_ = tile_src(c_, n_)
                for tt in range(4):
                    xs = (ti * 4 + tt) % 8
                    P.op("sync", (lambda xs, rr, src_: lambda e: e.dma_start(out=xt[xs][:], in_=src_[rr:rr + 128, :]))(xs, r0_ + tt * 128, src_),
                         writes=[("xt", xs)], dma="xt%d" % xs)

            xload(0)
            xload(1)
            tile_ctr = [0]
            for c in range(2 if "notiles" not in phases else 0):
                tiles = list(range(2, 10)) if c == 0 else list(range(12))
                if "fewtiles" in phases:
                    tiles = tiles[:2]
                if "c0only" in phases and c == 1:
                    continue
                if "c1only" in phases and c == 0:
                    continue
                prev_h = None
                for n in tiles:
                    own = 2 <= n <= 9
                    if own:
                        src, r0 = xo, c * 4096 + (n - 2) * 512
                    elif n < 2:
                        src, r0 = xh, n * 512
                    else:
                        src, r0 = xh, HALO + (n - 10) * 512
                    xTs = (c * 12 + n) % 2
                    ti = tile_ctr[0]
                    tile_ctr[0] += 1
                    for tt in range(4):
                        xs = (ti * 4 + tt) % 8
                        for kg in range(2):
                            b = cnt["tpb"] % 2
                            cnt["tpb"] += 1
                            for kk in range(4):
                                k = kg * 4 + kk
                                P.op("tensor", (lambda b, kk, xs, k: lambda e: e.transpose(out=ps[b][:, kk * 128:(kk + 1) * 128], in_=xt[xs][:, k * 128:(k + 1) * 128], identity=identf[:]))(b, kk, xs, k),
                                     reads=[("xt", xs), "identf"], writes=[PS(b)])
                            evac(xT[xTs][:, kg * 4:(kg + 1) * 4, tt * 128:(tt + 1) * 128],
                                 ps[b][:, :].rearrange("p (k t) -> p k t", k=4), b, writes=[("xT", xTs, tt, kg)])
                    xload(ti + 2)
                    need_uc = own or (c == 1 and n in (1, 10))
                    groups = []
                    if own:
                        groups += [("q", j) for j in range(4)]
                    groups += [("k", j) for j in range(4)] + [("v", j) for j in range(4)]
                    if need_uc:
                        hs = n % 3
                        gs = n % 2
                    for kind, j in groups:
                        col0 = {"q": 0, "k": 512, "v": 1024}[kind] + j * 128
                        b = 2 + cnt["pjb"] % 6
                        cnt["pjb"] += 1
                        for k in range(8):
                            P.op("tensor", (lambda b, k, col0, xTs: lambda e: e.matmul(ps[b][:, :], lhsT=win[:, k, col0:col0 + 128], rhs=xT[xTs][:, k, :], start=(k == 0), stop=(k == 7)))(b, k, col0, xTs),
                                 reads=[("win", col0 // 512)] + [("xT", xTs, t_, k // 4) for t_ in range(4)], writes=[PS(b)])
                        ss = cnt["st"] % 12
                        cnt["st"] += 1
                        evac(st[ss][:], ps[b][:, :], b, writes=[("st", ss)])
                        if kind == "q":
                            dst = qs[j, :, c * 4096 + (n - 2) * 512: c * 4096 + (n - 1) * 512]
                        elif kind == "k":
                            dst = ks[j, :, c * WIN + n * 512: c * WIN + (n + 1) * 512]
                        else:
                            dst = vs[j, :, c * WIN + n * 512: c * WIN + (n + 1) * 512]
                        P.op("sync", (lambda dst, ss: lambda e: e.dma_start(out=dst, in_=st[ss][:]))(dst, ss),
                             reads=[("st", ss)], dma="st%d" % ss)
                    if need_uc:
                        for j in range(4):
                            bgc = 2 + cnt["pjb"] % 6
                            cnt["pjb"] += 1
                            col0 = 2560 + j * 128
                            for k in range(8):
                                P.op("tensor", (lambda b, k, col0, xTs: lambda e: e.matmul(ps[b][:, :], lhsT=win[:, k, col0:col0 + 128], rhs=xT[xTs][:, k, :], start=(k == 0), stop=(k == 7)))(bgc, k, col0, xTs),
                                     reads=[("win", col0 // 512)] + [("xT", xTs, t_, k // 4) for t_ in range(4)], writes=[PS(bgc)])
                            gsl = cnt["gcs"] % 2
                            cnt["gcs"] += 1
                            P.op("scalar", (lambda gsl, b: lambda e: e.activation(out=gcs[gsl][:], in_=ps[b][:, :], func=AF.Copy))(gsl, bgc),
                                 writes=[PS(bgc), ("gcs", gsl)])
                            bu = 2 + cnt["pjb"] % 6
                            cnt["pjb"] += 1
                            col0 = 1536 + j * 128
                            for k in range(8):
                                P.op("tensor", (lambda b, k, col0, xTs: lambda e: e.matmul(ps[b][:, :], lhsT=win[:, k, col0:col0 + 128], rhs=xT[xTs][:, k, :], start=(k == 0), stop=(k == 7)))(bu, k, col0, xTs),
                                     reads=[("win", col0 // 512)] + [("xT", xTs, t_, k // 4) for t_ in range(4)], writes=[PS(bu)])
                            P.op("vector", (lambda hs, j, b, gsl: lambda e: e.tensor_tensor(out=hb[hs][:, j, 1:513], in0=ps[b][:, :], in1=gcs[gsl][:], op=ALU.mult))(hs, j, bu, gsl),
                                 reads=[("gcs", gsl)], writes=[PS(bu), ("hb", hs, j)])
                            if own:
                                bg = 2 + cnt["pjb"] % 6
                                cnt["pjb"] += 1
                                col0 = 2048 + j * 128
                                for k in range(8):
                                    P.op("tensor", (lambda b, k, col0, xTs: lambda e: e.matmul(ps[b][:, :], lhsT=win[:, k, col0:col0 + 128], rhs=xT[xTs][:, k, :], start=(k == 0), stop=(k == 7)))(bg, k, col0, xTs),
                                         reads=[("win", col0 // 512)] + [("xT", xTs, t_, k // 4) for t_ in range(4)], writes=[PS(bg)])
                                P.op("scalar", (lambda gs, j, b: lambda e: e.activation(out=gbb[gs][:, j, :], in_=ps[b][:, :], func=AF.Copy))(gs, j, bg),
                                     writes=[PS(bg), ("gbb", gs, j)])
                        hkeys = [("hb", hs, j) for j in range(4)]
                        if prev_h is not None and prev_h[0] == n - 1:
                            ph = prev_h[1]
                            P.op("vector", (lambda hs, ph: lambda e: e.tensor_copy(out=hb[hs][:, :, 0:1], in_=hb[ph][:, :, 512:513]))(hs, ph),
                                 reads=[("hb", ph, j) for j in range(4)], writes=[("hbg", hs)])
                            P.op("vector", (lambda hs, ph: lambda e: e.tensor_copy(out=hb[ph][:, :, 513:514], in_=hb[hs][:, :, 1:2]))(hs, ph),
                                 reads=hkeys, writes=[("hbg", ph)])
                            if 2 <= n - 1 <= 9:
                                conv_finalize(c, n - 1, ph, (n - 1) % 2)
                        else:
                            P.op("vector", (lambda hs: lambda e: e.memset(hb[hs][:, :, 0:1], 0.0))(hs), writes=[("hbg", hs)])
                        prev_h = (n, hs)
                        if c == 0 and n == 9:
                            P.op("vector", (lambda hs: lambda e: e.memset(hb[hs][:, :, 513:514], 0.0))(hs), writes=[("hbg", hs)])
                            conv_finalize(c, 9, hs, 9 % 2)
            P.flush()

        with ExitStack() as s2:
          if "1b" in phases:
            sb = lambda name, shape, dtp: s2.enter_context(nc.sbuf_tensor("b_" + name, shape, dtp))
            ps = [None] * 8
            for i in (0, 1, 4, 5, 6, 7):
                ps[i] = s2.enter_context(nc.psum_tensor("psb%d" % i, [128, 512], F32))
            psb16 = {i: s2.enter_context(nc.psum_tensor("psh%d" % i, [128, 1024], BF16)) for i in (2, 3)}
            et = sb("et", [128, 3 * 4, 2, 256], BF16)
            Qp = [sb("Qp%d" % i, [128, 2, 4096], BF16) for i in range(2)]
            Kp = [sb("Kp%d" % i, [128, WIN], BF16) for i in range(2)]
            Vp = [sb("Vp%d" % i, [128, WIN], BF16) for i in range(2)]
            acc = [sb("acc%d" % i, [128, 4096], F32) for i in range(2)]
            rec = sb("rec", [128, 1024], F32)
            At = [sb("At%d" % i, [128, 4096], BF16) for i in range(2)]
            PT = [sb("PT%d" % i, [128, 2, 256], BF16) for i in range(8)]
            Va = [sb("Va%d" % i, [128, 384], BF16) for i in range(8)]
            P.op("gpsimd", lambda e: e.dma_start(out=et[:].rearrange("p a h n -> p (a h n)"), in_=etab_d), writes=["et"], dma="c_et")
            for i in range(2):
                P.op("gpsimd", (lambda i: lambda e: e.memset(Qp[i][64:128, 0, :], 0.0))(i), writes=[("QpZ", i)])
                P.op("gpsimd", (lambda i: lambda e: e.memset(Qp[i][0:64, 1, :], 0.0))(i), writes=[("QpZ", i)])
            for i in range(8):
                P.op("vector", (lambda i: lambda e: e.memset(Va[i][:, 64:192], 1.0))(i), writes=[("VaO", i)])
            bsmall = "bsmall" in phases
            if bsmall:
                for i in range(2):
                    P.op("vector", (lambda i: lambda e: e.memset(acc[i][:], 1.0))(i), writes=[("acc", i)])
            units = []
            nlp = 0
            nblk_ctr = 0
            for c in range(1 if bsmall else 2):
                for hp in range(1 if bsmall else 4):
                    lp = nlp % 2
                    nlp += 1
                    first_of_hp = True
                    for di, d in enumerate(CFG):
                        Q = 512 if d < 16 else 256
                        nblk = (4096 // d) // Q
                        ntile = Q // 128 + 1
                        mb_base = c * NMB + (0, 33, 69)[di]
                        tiles_per_r = (33, 9, 3)[di]
                        for r in range(1 if bsmall else d):
                            for m in range(1 if bsmall else nblk):
                                qi0 = HALO // d + m * Q
                                bsel = nblk_ctr % 2
                                nblk_ctr += 1
                                for jj in range(ntile):
                                    k0 = qi0 - 64 + 128 * jj
                                    jglob = (qi0 - HALO // d) // 128 + jj
                                    if jj == 0:
                                        c0, N, e0 = 0, 128, 128
                                    elif jj == ntile - 1:
                                        c0, N, e0 = Q - 128, 128, 0
                                    else:
                                        c0, N, e0 = 128 * (jj - 1), 256, 0
                                    units.append(dict(c=c, hp=hp, lp=lp, di=di, d=d, r=r, Q=Q, ntile=ntile, jj=jj, c0=c0, N=N, e0=e0,
                                                      mbc=mb_base + r * tiles_per_r + jglob, kw=k0 * d + r, qo=(qi0 + c0) * d + r - HALO,
                                                      q0=qi0 * d + r - HALO, pa=4 + 2 * bsel, pb=5 + 2 * bsel,
                                                      first_of_hp=first_of_hp, last_of_blk=(jj == ntile - 1), last_of_hp=False))
                                    first_of_hp = False
                    units[-1]["last_of_hp"] = True

            def front(u, U):
                lp, d = U["lp"], U["d"]
                if U["first_of_hp"]:
                    c, hp = U["c"], U["hp"]
                    for h2 in range(2):
                        P.op("sync", (lambda lp, c, hp, h2: lambda e: e.dma_start(out=Qp[lp][h2 * 64:(h2 + 1) * 64, h2, :], in_=qs[hp, h2 * 64:(h2 + 1) * 64, c * 4096:(c + 1) * 4096]))(lp, c, hp, h2), writes=[("Qp", lp)], dma="ld_q%d" % lp)
                    P.op("sync", (lambda lp, c, hp: lambda e: e.dma_start(out=Kp[lp][:], in_=ks[hp, :, c * WIN:(c + 1) * WIN]))(lp, c, hp), writes=[("Kp", lp)], dma="ld_k%d" % lp)
                    P.op("sync", (lambda lp, c, hp: lambda e: e.dma_start(out=Vp[lp][:], in_=vs[hp, :, c * WIN:(c + 1) * WIN]))(lp, c, hp), writes=[("Vp", lp)], dma="ld_v%d" % lp)
                sbk, tbk, vsl, psl = u % 2, 2 + u % 2, u % 8, u % 8
                kw, qo, N, mbc, di, hp, e0 = U["kw"], U["qo"], U["N"], U["mbc"], U["di"], U["hp"], U["e0"]
                P.op("tensor", (lambda tbk, lp, kw, d: lambda e: e.transpose(out=psb16[tbk][:, 0:128], in_=Vp[lp][:, sl(kw, 128, d)], identity=identb[:]))(tbk, lp, kw, d),
                     reads=[("Vp", lp), "identb"], writes=[PS(tbk)])
                if u % 8 == 7:
                    P.op("vector", (lambda vsl, tbk: lambda e: e.tensor_copy(out=Va[vsl][:, :].rearrange("p (a b) -> p a b", b=192)[:, :, 0:64],
                                                                         in_=psb16[tbk][:, 0:128].rearrange("p (a b) -> p a b", b=64)))(vsl, tbk),
                         writes=[PS(tbk), ("Va", vsl)])
                else:
                    P.op("scalar", (lambda vsl, tbk: lambda e: e.activation(out=Va[vsl][:, :].rearrange("p (a b) -> p a b", b=192)[:, :, 0:64],
                                                                         in_=psb16[tbk][:, 0:128].rearrange("p (a b) -> p a b", b=64), func=AF.Copy))(vsl, tbk),
                         writes=[PS(tbk), ("Va", vsl)])
                P.op("tensor", (lambda sbk, lp, kw, qo, N, d: lambda e: e.matmul(
                    ps[sbk][:, 0:2 * N].rearrange("p (h n) -> p h n", h=2), lhsT=Kp[lp][:, sl(kw, 128, d)],
                    rhs=Qp[lp][:, :, sl(qo, N, d)], start=True, stop=True))(sbk, lp, kw, qo, N, d),
                    reads=[("Kp", lp), ("Qp", lp), ("QpZ", lp)], writes=[PS(sbk)])
                P.op("scalar", (lambda psl, sbk, N, mbc: lambda e: e.activation(
                    out=PT[psl][:, :, 0:N], in_=ps[sbk][:, 0:2 * N].rearrange("p (h n) -> p h n", h=2),
                    func=AF.Exp, bias=mbt[:, mbc:mbc + 1], scale=0.125))(psl, sbk, N, mbc),
                    reads=["mbt"], writes=[PS(sbk), ("PT", psl)])
                P.op("vector" if u % 4 == 3 else "gpsimd", (lambda psl, N, di, hp, e0: lambda e: e.tensor_tensor(
                    out=PT[psl][:, :, 0:N], in0=PT[psl][:, :, 0:N], in1=et[:, di * 4 + hp, :, e0:e0 + N], op=ALU.mult))(psl, N, di, hp, e0),
                    reads=["et"], writes=[("PT", psl)])

            def back(u, U):
                vsl, psl = u % 8, u % 8
                c0, N, jj, ntile, d, Q, q0, di = U["c0"], U["N"], U["jj"], U["ntile"], U["d"], U["Q"], U["q0"], U["di"]
                for h2, pacc in ((0, U["pa"]), (1, U["pb"])):
                    P.op("tensor", (lambda pacc, h2, vsl, psl, c0, N, jj, ntile: lambda e: e.matmul(
                        ps[pacc][:, c0:c0 + N], lhsT=Va[vsl][:, h2 * 128:(h2 + 1) * 128], rhs=PT[psl][:, h2, 0:N],
                        start=(jj == 0), stop=(jj == ntile - 1), skip_group_check=True))(pacc, h2, vsl, psl, c0, N, jj, ntile),
                        reads=[("Va", vsl), ("VaO", vsl), ("PT", psl)], writes=[PS(pacc)])
                if U["last_of_blk"]:
                    for h2, pacc in ((0, U["pa"]), (1, U["pb"])):
                        if di == 0:
                            P.op("vector", (lambda h2, pacc, q0, Q, d: lambda e: e.tensor_copy(out=acc[h2][:, sl(q0, Q, d)], in_=ps[pacc][:, 0:Q]))(h2, pacc, q0, Q, d),
                                 writes=[PS(pacc), ("acc", h2)])
                        else:
                            P.op("vector", (lambda h2, pacc, q0, Q, d: lambda e: e.tensor_tensor(out=acc[h2][:, sl(q0, Q, d)], in0=ps[pacc][:, 0:Q], in1=acc[h2][:, sl(q0, Q, d)], op=ALU.add))(h2, pacc, q0, Q, d),
                                 writes=[PS(pacc), ("acc", h2)])
                if U["last_of_hp"]:
                    c, hp = U["c"], U["hp"]
                    asl = U["lp"]
                    for cc in range(4):
                        csl = slice(cc * 1024, (cc + 1) * 1024)
                        P.op("scalar", (lambda csl: lambda e: e.activation(out=rec[0:64, :], in_=acc[0][64:128, csl], func=AF.Ln))(csl), reads=[("acc", 0)], writes=["recA"])
                        P.op("scalar", lambda e: e.activation(out=rec[0:64, :], in_=rec[0:64, :], func=AF.Exp, scale=-1.0), writes=["recA"])
                        P.op("vector", (lambda asl, csl: lambda e: e.tensor_tensor(out=At[asl][0:64, csl], in0=acc[0][0:64, csl], in1=rec[0:64, :], op=ALU.mult))(asl, csl),
                             reads=[("acc", 0), "recA"], writes=[("At", asl)])
                        P.op("scalar", (lambda csl: lambda e: e.activation(out=rec[64:128, :], in_=acc[1][0:64, csl], func=AF.Ln))(csl), reads=[("acc", 1)], writes=["recB"])
                        P.op("scalar", lambda e: e.activation(out=rec[64:128, :], in_=rec[64:128, :], func=AF.Exp, scale=-1.0), writes=["recB"])
                        P.op("vector", (lambda asl, csl: lambda e: e.tensor_tensor(out=At[asl][64:128, csl], in0=acc[1][64:128, csl], in1=rec[64:128, :], op=ALU.mult))(asl, csl),
                             reads=[("acc", 1), "recB"], writes=[("At", asl)])
                    P.op("sync", (lambda asl, hp, c: lambda e: e.dma_start(out=ats[hp, :, c * 4096:(c + 1) * 4096], in_=At[asl][:]))(asl, hp, c),
                         reads=[("At", asl)], dma="at%d" % asl)

            SKEW = 4
            for i in range(len(units) + SKEW):
                if i < len(units):
                    front(i, units[i])
                if i - SKEW >= 0:
                    back(i - SKEW, units[i - SKEW])
            P.flush()

        with ExitStack() as s3:
          if "2" in phases:
            sb = lambda name, shape, dtp: s3.enter_context(nc.sbuf_tensor("c_" + name, shape, dtp))
            ps = [s3.enter_context(nc.psum_tensor("psc%d" % i, [128, 512], F32)) for i in range(8)]
            wd = sb("wd", [128, NF, D], BF16)
            wo = sb("wo", [128, 8, D], BF16)
            lnp = sb("lnp", [128, 4, D], F32)
            wr = [sb("wr%d" % i, [128, 2, 8, 128], BF16) for i in range(4)]
            A2 = [sb("A2_%d" % i, [128, 4, 512], BF16) for i in range(2)]
            Y2 = [sb("Y2_%d" % i, [128, 4, 512], BF16) for i in range(2)]
            sq = sb("sq", [128, 8, 512], BF16)
            ones = sb("ones", [128, 2], BF16)
            x2 = [sb("x2_%d" % i, [128, D], F32) for i in range(3)]
            x1 = sb("x1", [128, 6, D], F32)
            zb = sb("zb", [128, D], F32)
            zb2 = sb("zb2", [128, D], F32)
            wof = zb2
            x1T = sb("x1T", [128, 8, 512], BF16)
            hT = sb("hT", [128, NF, 512], BF16)
            sg = [sb("sg%d" % i, [128, 512], F32) for i in range(2)]
            ob = [sb("ob%d" % i, [128, D], F32) for i in range(2)]
            rs = sb("rs", [128, 8], F32)
            stats = [sb("stats%d" % i, [128, 2, 6], F32) for i in range(2)]
            mv = [sb("mv%d" % i, [128, 2], F32) for i in range(2)]
            sc = [sb("sc%d" % i, [128, 4], F32) for i in range(2)]

            P.op("vector", lambda e: e.memset(ones[:], 1.0), writes=["ones"])
            def phase2_setup_rest():
                for j in range(8):
                    stg, skeys = (zb, [("zb", 0), ("zb", 1)]) if j % 2 == 0 else (zb2, [("zb2", 0), ("zb2", 1)])
                    P.op("sync", (lambda j, stg: lambda e: e.dma_start(out=stg[:], in_=w_o_d[j * 128:(j + 1) * 128, :]))(j, stg), writes=skeys, dma="c_wo" if j % 2 == 0 else "c_wo2")
                    P.op("vector", (lambda j, stg: lambda e: e.tensor_scalar(out=wo[:, j, :], in0=stg[:], scalar1=gct[:, j:j + 1], scalar2=None, op0=ALU.mult))(j, stg),
                         reads=skeys + ["gct"], writes=[("wo", j)])
                P.op("sync", lambda e: e.dma_start(out=lnp[:].rearrange("p a n -> p (a n)"), in_=lnp_d), writes=["lnp"], dma="c_ln")
                for f0 in range(0, NF, 2):
                    P.op("gpsimd", (lambda f0: lambda e: e.dma_start(out=wd[:, f0:f0 + 2, :], in_=w_down_d[f0 * 128:(f0 + 2) * 128, :].rearrange("(f p) n -> p f n", p=128)))(f0),
                         writes=["wd"], dma="c_wd")

            cnt = {"x2": 0, "wr": 0, "gb": 0, "db": 0, "ob": 0, "alt": 0}
            EPS1 = LN_EPS / (ALPHA * ALPHA)

            deferred = []

            def defer(n, fn):
                deferred.append([n, fn])

            def tick():
                for it_ in deferred:
                    it_[0] -= 1
                due = [it_ for it_ in deferred if it_[0] <= 0]
                for it_ in due:
                    deferred.remove(it_)
                    it_[1]()

            def layernorm(src, dst, gi, eps, src_keys, dst_key, gb_eng, si, apply_gb=True, pre=None, post=None, delay=0):
                st_, mv_, sc_ = stats[si], mv[si], sc[si]
                kk = lambda n: (n, si)

                def stage_a():
                    if pre is not None:
                        pre()
                    for hh in range(2):
                        P.op("vector", (lambda hh: lambda e: e.bn_stats(out=st_[:, hh, :], in_=src[:, hh * 512:(hh + 1) * 512]))(hh),
                             reads=src_keys, writes=[kk("stats%d" % hh)])
                    P.op("vector", lambda e: e.bn_aggr(out=mv_[:], in_=st_[:].rearrange("p a s -> p (a s)")), reads=[kk("stats0"), kk("stats1")], writes=[kk("mv")])
                    P.op("vector", lambda e: e.tensor_scalar(out=sc_[:, 0:1], in0=mv_[:, 1:2], scalar1=eps, scalar2=None, op0=ALU.add), reads=[kk("mv")], writes=[kk("sc0")])

                def stage_b():
                    P.op("scalar", lambda e: e.activation(out=sc_[:, 1:2], in_=sc_[:, 0:1], func=AF.Sqrt), reads=[kk("sc0")], writes=[kk("sc1")])

                def stage_c():
                    P.op("vector", lambda e: e.reciprocal(out=sc_[:, 2:3], in_=sc_[:, 1:2]), reads=[kk("sc1")], writes=[kk("sc2")])
                    P.op("vector", lambda e: e.scalar_tensor_tensor(out=sc_[:, 3:4], in0=mv_[:, 0:1], scalar=-1.0, in1=sc_[:, 2:3], op0=ALU.mult, op1=ALU.mult), reads=[kk("mv"), kk("sc2")], writes=[kk("sc3")])
                    P.op("vector", lambda e: e.tensor_scalar(out=dst, in0=src, scalar1=sc_[:, 2:3], scalar2=sc_[:, 3:4], op0=ALU.mult, op1=ALU.add), reads=[kk("sc2"), kk("sc3")] + list(src_keys), writes=[dst_key])
                    if apply_gb:
                        P.op(gb_eng, lambda e: e.tensor_tensor(out=dst, in0=dst, in1=lnp[:, gi, :], op=ALU.mult), reads=["lnp"], writes=[dst_key])
                        P.op(gb_eng, lambda e: e.tensor_tensor(out=dst, in0=dst, in1=lnp[:, gi + 1, :], op=ALU.add), reads=["lnp"], writes=[dst_key])
                    if post is not None:
                        post()

                if delay < 0:
                    stage_a(); stage_b(); stage_c()
                else:
                    if delay == 0:
                        stage_a()
                    else:
                        defer(delay, stage_a)
                    defer(delay + 1, stage_b)
                    defer(delay + 2, stage_c)

            NB = NOWN // 512 if "p2small" not in phases else 2
            x1T_keys = [("x1T", tt) for tt in range(4)]

            def front_start(tb):
                g0, bs = tb * 512, tb % 2
                for j in range(4):
                    P.op("sync", (lambda j, bs, g0: lambda e: e.dma_start(out=A2[bs][:, j, :], in_=ats[j, :, g0:g0 + 512]))(j, bs, g0), writes=[("A2", bs)], dma="a2_%d" % bs)
                    P.op("sync", (lambda j, bs, g0: lambda e: e.dma_start(out=Y2[bs][:, j, :], in_=ys[j, :, g0:g0 + 512]))(j, bs, g0), writes=[("Y2", bs)], dma="y2_%d" % bs)
                x2_load(tb, 0)
                x2_load(tb, 1)

            def front_sq(tb):
                bs = tb % 2
                P.op("scalar", (lambda bs: lambda e: e.activation(out=sq[:, 0:4, :], in_=A2[bs][:], func=AF.Square))(bs), reads=[("A2", bs)], writes=["sqA"])
                P.op("gpsimd", (lambda bs: lambda e: e.tensor_tensor(out=sq[:, 4:8, :], in0=Y2[bs][:], in1=Y2[bs][:], op=ALU.mult))(bs), reads=[("Y2", bs)], writes=["sqB"])

            def front_ssq(tb):
                for tt in range(4):
                    for ab in range(2):
                        for j in range(4):
                            P.op("tensor", (lambda tt, ab, j: lambda e: e.matmul(ps[3][:, tt * 2 + ab:tt * 2 + ab + 1], lhsT=sq[:, ab * 4 + j, tt * 128:(tt + 1) * 128], rhs=ones[:, 0:1], start=(j == 0), stop=(j == 3)))(tt, ab, j),
                                 reads=["sqA" if ab == 0 else "sqB", "ones"], writes=[PS(3)])
                P.op("vector", lambda e: e.tensor_scalar(out=rs[:], in0=ps[3][:, 0:8], scalar1=ALPHA * ALPHA / 512.0, scalar2=LN_EPS * ALPHA * ALPHA, op0=ALU.mult, op1=ALU.add), writes=[PS(3), "rs"])
                P.op("scalar", lambda e: e.activation(out=rs[:], in_=rs[:], func=AF.Sqrt), writes=["rs"])
                P.op("vector", lambda e: e.reciprocal(out=rs[:], in_=rs[:]), writes=["rs"])

            def x2_load(tb, tt):
                xs = (tb * 4 + tt) % 3
                g = tb * 512 + tt * 128
                P.op("sync", (lambda xs, g: lambda e: e.dma_start(out=x2[xs][:], in_=xo[g:g + 128, :]))(xs, g), writes=[("x2", xs)], dma="x2_%d" % xs)

            def front_mix(tb, tt, immediate=False):
                g0, bs = tb * 512, tb % 2
                xs = (tb * 4 + tt) % 3
                for ab, src2 in ((0, A2[bs]), (1, Y2[bs])):
                    for half in range(2):
                        b = ab * 2 + half
                        for j in range(4):
                            P.op("tensor", (lambda b, src2, j, tt, ab, half: lambda e: e.matmul(ps[b][:, :], lhsT=src2[:, j, tt * 128:(tt + 1) * 128], rhs=wo[:, ab * 4 + j, half * 512:(half + 1) * 512], start=(j == 0), stop=(j == 3)))(b, src2, j, tt, ab, half),
                                 reads=[("A2", bs) if ab == 0 else ("Y2", bs), ("wo", ab * 4 + j)], writes=[PS(b)])
                def stt():
                    for half in range(2):
                        hsl = slice(half * 512, (half + 1) * 512)
                        P.op("vector", (lambda half, hsl, xs, tt: lambda e: e.scalar_tensor_tensor(out=zb[:, hsl], in0=ps[half][:, :], scalar=rs[:, 2 * tt:2 * tt + 1], in1=x2[xs][:, hsl], op0=ALU.mult, op1=ALU.add))(half, hsl, xs, tt),
                             reads=["rs", ("x2", xs)], writes=[PS(half), ("zb", half)])
                        P.op("vector", (lambda half, hsl, tt: lambda e: e.scalar_tensor_tensor(out=zb[:, hsl], in0=ps[2 + half][:, :], scalar=rs[:, 2 * tt + 1:2 * tt + 2], in1=zb[:, hsl], op0=ALU.mult, op1=ALU.add))(half, hsl, tt),
                             reads=["rs"], writes=[PS(2 + half), ("zb", half)])
                    if tt + 2 < 4:
                        x2_load(tb, tt + 2)
                xsl = (tb * 4 + tt) % 6
                layernorm(zb[:], x1[:, xsl, :], 0, EPS1, [("zb", 0), ("zb", 1)], ("x1", xsl), "gpsimd", 0, apply_gb=False, pre=stt, delay=(-1 if immediate else 1))

            def front_tr(tb, tt):
                xsl = (tb * 4 + tt) % 6
                for kg in range(2):
                    b = 6 + kg
                    for kk in range(4):
                        k = kg * 4 + kk
                        P.op("tensor", (lambda b, kk, k, xsl: lambda e: e.transpose(out=ps[b][:, kk * 128:(kk + 1) * 128], in_=x1[:, xsl, k * 128:(k + 1) * 128], identity=identf[:]))(b, kk, k, xsl),
                             reads=[("x1", xsl), "identf"], writes=[PS(b)])
                    for kk in range(4):
                        k = kg * 4 + kk
                        cnt["alt"] += 1
                        if cnt["alt"] % 2:
                            P.op("scalar", (lambda b, kk, k, tt: lambda e: e.activation(out=x1T[:, k, tt * 128:(tt + 1) * 128], in_=ps[b][:, kk * 128:(kk + 1) * 128], func=AF.Identity, bias=lnc[:, 8 + k:9 + k], scale=lnc[:, k:k + 1]))(b, kk, k, tt),
                                 reads=["lnc"], writes=[PS(b), ("x1T", tt)])
                        else:
                            P.op("vector", (lambda b, kk, k, tt: lambda e: e.tensor_scalar(out=x1T[:, k, tt * 128:(tt + 1) * 128], in0=ps[b][:, kk * 128:(kk + 1) * 128], scalar1=lnc[:, k:k + 1], scalar2=lnc[:, 8 + k:9 + k], op0=ALU.mult, op1=ALU.add))(b, kk, k, tt),
                                 reads=["lnc"], writes=[PS(b), ("x1T", tt)])
                P.op("gpsimd", (lambda xsl: lambda e: e.tensor_tensor(out=x1[:, xsl, :], in0=x1[:, xsl, :], in1=lnp[:, 0, :], op=ALU.mult))(xsl), reads=["lnp"], writes=[("x1", xsl)])
                P.op("gpsimd", (lambda xsl: lambda e: e.tensor_tensor(out=x1[:, xsl, :], in0=x1[:, xsl, :], in1=lnp[:, 1, :], op=ALU.add))(xsl), reads=["lnp"], writes=[("x1", xsl)])

            def wload(gidx):
                if gidx >= NB * NF:
                    return
                ws, f = gidx % 4, gidx % NF
                P.op("sync", (lambda ws, f: lambda e: e.dma_start(out=wr[ws][:].rearrange("p g k n -> p (g k n)"), in_=wgu[f, :, :]))(ws, f), writes=[("wr", ws)], dma="wr%d" % ws)

            def gateup(tb, hooks=()):
                for f in range(NF):
                    for hf, hfn in hooks:
                        if hf == f:
                            hfn()
                    gidx = tb * NF + f
                    ws = gidx % 4
                    gbk = 4 + 2 * (gidx % 2)
                    for g in range(2):
                        for k in range(8):
                            P.op("tensor", (lambda gbk, g, k, ws: lambda e: e.matmul(ps[gbk + g][:, :], lhsT=wr[ws][:, g, k, :], rhs=x1T[:, k, :], start=(k == 0), stop=(k == 7)))(gbk, g, k, ws),
                                 reads=[("wr", ws)] + x1T_keys, writes=[PS(gbk + g)])
                    wload(gidx + 4)
                    sgs = f % 2
                    P.op("scalar", (lambda sgs, gbk: lambda e: e.activation(out=sg[sgs][:], in_=ps[gbk][:, :], func=AF.Silu))(sgs, gbk), writes=[PS(gbk), ("sg", sgs)])
                    P.op("vector", (lambda sgs, gbk, f: lambda e: e.tensor_tensor(out=hT[:, f, :], in0=ps[gbk + 1][:, :], in1=sg[sgs][:], op=ALU.mult))(sgs, gbk, f),
                         reads=[("sg", sgs)], writes=[PS(gbk + 1), ("hT", f)])
                    tick()

            def down(tb, tt):
                g0 = tb * 512
                for half in range(2):
                    b = 4 + half
                    xsl = (tb * 4 + tt) % 6
                    hsl = slice(half * 512, (half + 1) * 512)
                    for f in range(NF):
                        P.op("tensor", (lambda b, f, tt, half: lambda e: e.matmul(ps[b][:, :], lhsT=hT[:, f, tt * 128:(tt + 1) * 128], rhs=wd[:, f, half * 512:(half + 1) * 512], start=(f == 0), stop=(f == NF - 1)))(b, f, tt, half),
                             reads=[("hT", f), "wd"], writes=[PS(b)])
                        if f == 10 or f == NF - 1:
                            tick()
                    P.op("vector", (lambda b, hsl, xsl: lambda e: e.scalar_tensor_tensor(out=zb2[:, hsl], in0=x1[:, xsl, hsl], scalar=ALPHA, in1=ps[b][:, :], op0=ALU.mult, op1=ALU.add))(b, hsl, xsl),
                         reads=[("x1", xsl)], writes=[PS(b), ("zb2", half)])

            def down_ln(tb, tt):
                g = tb * 512 + tt * 128
                os_ = cnt["ob"] % 2
                cnt["ob"] += 1

                def store():
                    P.op("sync", (lambda os_, g: lambda e: e.dma_start(out=yo[g:g + 128, :], in_=ob[os_][:]))(os_, g), reads=[("ob", os_)], dma="o2_%d" % os_)
                layernorm(zb2[:], ob[os_][:], 2, LN_EPS, [("zb2", 0), ("zb2", 1)], ("ob", os_), "gpsimd", 1, post=store, delay=0)

            front_start(0)
            for g in range(4):
                wload(g)
            phase2_setup_rest()
            front_sq(0)
            front_ssq(0)
            for tt in range(4):
                front_mix(0, tt, immediate=True)
                front_tr(0, tt)
            for tb in range(NB):
                nxt = tb + 1 < NB
                if nxt:
                    gateup(tb, hooks=((1, (lambda tb: lambda: front_start(tb + 1))(tb)), (6, (lambda tb: lambda: front_sq(tb + 1))(tb)), (9, (lambda tb: lambda: front_ssq(tb + 1))(tb)),
                                      (13, (lambda tb: lambda: front_mix(tb + 1, 0))(tb)), (18, (lambda tb: lambda: front_mix(tb + 1, 1))(tb))))
                else:
                    gateup(tb)
                for tt in range(4):
                    down(tb, tt)
                    if nxt:
                        if tt < 2:
                            front_mix(tb + 1, tt + 2)
                        front_tr(tb + 1, tt)
                    down_ln(tb, tt)
            while deferred:
                tick()
            P.flush()
        P.flush(final=True)
    return nc


def _etab():
    slopes = np.array([2.0 ** (-8.0 * (h + 1) / 8) for h in range(8)], dtype=np.float64)
    a = np.arange(128)[:, None]
    b = np.arange(256)[None, :]
    band = ((b - a) >= 0) & ((b - a) <= 128)
    dist = np.abs(b - a - 64).astype(np.float64)
    out = np.zeros((128, 3, 4, 2, 256), dtype=np.float32)
    for di, d in enumerate(CFG):
        for hp in range(4):
            for h2 in range(2):
                out[:, di, hp, h2, :] = np.where(band, np.exp(-slopes[2 * hp + h2] * d * dist), 0.0)
    return np.ascontiguousarray(out.reshape(128, -1))


def _maskbias(valid):
    mb = np.zeros((128, 2 * NMB), dtype=np.float32)
    a = np.arange(128)
    for c in range(2):
        col = c * NMB
        for d, (k00, ntile) in zip(CFG, ((960, 33), (192, 9), (0, 3))):
            for r in range(d):
                for j in range(ntile):
                    w = (k00 + 128 * j + a) * d + r
                    mb[:, col] = np.where(valid[c][w], 0.0, NEGB)
                    col += 1
        assert col == (c + 1) * NMB
    return mb


_NC_CACHE = {}


def kernel(x_prompt, x_sample, w_in, conv_w, g_attn, g_conv, w_o, ln1_g, ln1_b, w_gate, w_up, w_down, ln2_g, ln2_b):
    f32 = lambda a: np.ascontiguousarray(np.asarray(a, dtype=np.float32))
    x_prompt, x_sample = f32(x_prompt), f32(x_sample)
    S2 = x_sample.shape[1]
    etab = _etab()
    ident = np.eye(128, dtype=np.float32)
    cw = f32(np.asarray(conv_w).reshape(3, 4, 128).transpose(2, 1, 0).reshape(128, 12))
    gcol = f32(np.concatenate([np.asarray(g_attn), np.asarray(g_conv)]).reshape(8, 128).T)
    lnp = f32(np.broadcast_to(np.stack([np.asarray(ln1_g), np.asarray(ln1_b), np.asarray(ln2_g), np.asarray(ln2_b)]).reshape(1, 4 * D), (128, 4 * D)))
    lncol = f32(np.concatenate([np.asarray(ln1_g).reshape(8, 128), np.asarray(ln1_b).reshape(8, 128)], axis=0).T)
    shared = {"lncol": lncol, "etab": etab, "ident": ident, "w_in": f32(w_in), "w_o": f32(w_o), "w_gate": f32(w_gate), "w_up": f32(w_up),
              "w_down": f32(w_down), "cw": cw, "gcol": gcol, "lnp": lnp}
    in_maps = []
    for c in range(8):
        s, qd = c // 4, c % 4
        own0 = 4096 * qd
        xo = np.concatenate([x_prompt[c], x_sample[s, own0:own0 + 4096]], axis=0)
        xh = np.zeros((2 * HALO, D), dtype=np.float32)
        valid = np.zeros((2, WIN), dtype=bool)
        valid[0, HALO:HALO + 4096] = True
        valid[1, HALO:HALO + 4096] = True
        if own0 - HALO >= 0:
            xh[0:HALO] = x_sample[s, own0 - HALO:own0]
            valid[1, 0:HALO] = True
        if own0 + 4096 + HALO <= S2:
            xh[HALO:] = x_sample[s, own0 + 4096:own0 + 4096 + HALO]
            valid[1, HALO + 4096:] = True
        m = dict(shared)
        m.update({"xo": np.ascontiguousarray(xo), "xh": xh, "mb": _maskbias(valid)})
        in_maps.append(m)
    if "nc" not in _NC_CACHE:
        _NC_CACHE["nc"] = build_nc()
    res = run_bass_kernel_spmd(_NC_CACHE["nc"], in_maps, core_ids=list(range(8)))
    y_prompt = np.empty_like(x_prompt)
    y_sample = np.empty_like(x_sample)
    for c in range(8):
        yo = np.asarray(res.results[c]["yo"], dtype=np.float32)
        y_prompt[c] = yo[0:4096]
        s, qd = c // 4, c % 4
        y_sample[s, 4096 * qd:4096 * (qd + 1)] = yo[4096:]
    return (y_prompt, y_sample)
```
